# Optimizing a Trainium2 kernel written in Bass

```python
import jax, jax.numpy as jnp
from jax import lax
import numpy as np

D_MODEL = 1024
BATCH = 8
SEQ = 4096
DEPTH = 4

N_HEADS = 16
HEAD_DIM = D_MODEL // N_HEADS
D_FF = ((8 * D_MODEL // 3 + 127) // 128) * 128
CONV_WIDTH = 3
Q_BLOCK = 128
N_MIXERS = 2
N_MOD = 6
RMS_EPS = 1e-6
N_SB_LAYERS = (DEPTH + 1) // 2
N_FOX_LAYERS = DEPTH // 2

kernel_name = "hybrid_stickbreak_fox_convffn"


def rmsnorm(x, g):
    xf = x.astype(jnp.float32)
    y = xf * lax.rsqrt(jnp.mean(xf * xf, axis=-1, keepdims=True) + RMS_EPS)
    return (y * g.astype(jnp.float32)).astype(x.dtype)


def modulate(h, shift, scale):
    return h * (1.0 + scale[:, None, :]) + shift[:, None, :]


def split_heads(t):
    b, s, _ = t.shape
    return t.reshape(b, s, N_HEADS, HEAD_DIM).transpose(0, 2, 1, 3)


def merge_heads(t):
    b, h, s, d = t.shape
    return t.transpose(0, 2, 1, 3).reshape(b, s, h * d)


def stick_breaking_attention(q, k, v):
    s_len = q.shape[2]
    scale = HEAD_DIM ** -0.5
    outs = []
    for blk in range(s_len // Q_BLOCK):
        q0 = blk * Q_BLOCK
        k_end = q0 + Q_BLOCK
        qb, kb, vb = q[:, :, q0:k_end], k[:, :, :k_end], v[:, :, :k_end]
        z = jnp.einsum('bhqd,bhkd->bhqk', qb, kb).astype(jnp.float32) * scale
        t_idx = q0 + jnp.arange(Q_BLOCK)[:, None]
        s_idx = jnp.arange(k_end)[None, :]
        strict = s_idx < t_idx
        log_beta = jax.nn.log_sigmoid(z)
        log_1mb = jnp.where(strict, jax.nn.log_sigmoid(-z), 0.0)
        after = lax.cumsum(log_1mb, axis=3, reverse=True) - log_1mb
        a = jnp.where(strict, jnp.exp(log_beta + after), 0.0)
        outs.append(jnp.einsum('bhqk,bhkd->bhqd', a.astype(vb.dtype), vb))
    return jnp.concatenate(outs, axis=2)


def forgetting_attention(q, k, v, log_f):
    s_len = q.shape[2]
    scale = HEAD_DIM ** -0.5
    cum = lax.cumsum(log_f, axis=2)
    outs = []
    for blk in range(s_len // Q_BLOCK):
        q0 = blk * Q_BLOCK
        k_end = q0 + Q_BLOCK
        qb, kb, vb = q[:, :, q0:k_end], k[:, :, :k_end], v[:, :, :k_end]
        z = jnp.einsum('bhqd,bhkd->bhqk', qb, kb).astype(jnp.float32) * scale
        decay = cum[:, :, q0:k_end, None] - cum[:, :, None, :k_end]
        t_idx = q0 + jnp.arange(Q_BLOCK)[:, None]
        s_idx = jnp.arange(k_end)[None, :]
        logits = jnp.where(s_idx <= t_idx, z + decay, -jnp.inf)
        p = jax.nn.softmax(logits, axis=-1)
        outs.append(jnp.einsum('bhqk,bhkd->bhqd', p.astype(vb.dtype), vb))
    return jnp.concatenate(outs, axis=2)


def causal_depthwise_conv(h, w, b):
    s_len = h.shape[1]
    hp = jnp.pad(h, ((0, 0), (CONV_WIDTH - 1, 0), (0, 0)))
    out = b[None, None, :]
    for kk in range(CONV_WIDTH):
        out = out + w[kk][None, None, :] * hp[:, kk:kk + s_len]
    return out


def setup_inputs(seed: int = 0) -> dict:
    key = jax.random.key(seed)
    ks = jax.random.split(key, 20)
    f32 = jnp.float32
    d, f, h = D_MODEL, D_FF, N_HEADS

    def nrm(k, shape, s):
        return jax.random.normal(k, shape, f32) * s

    return {
        "x": nrm(ks[0], (BATCH, SEQ, d), 1.0),
        "c": nrm(ks[1], (BATCH, d), 1.0),
        "w_mod": nrm(ks[2], (DEPTH, d, N_MOD * d), 0.5 * d ** -0.5),
        "b_mod": nrm(ks[3], (DEPTH, N_MOD * d), 0.02),
        "g_mix_pre": 1.0 + nrm(ks[4], (DEPTH, d), 0.05),
        "g_mix_post": 1.0 + nrm(ks[5], (DEPTH, d), 0.05),
        "w_qkv": nrm(ks[6], (DEPTH, d, 3 * d), d ** -0.5),
        "w_o": nrm(ks[7], (DEPTH, d, d), d ** -0.5),
        "w_fg": nrm(ks[8], (N_FOX_LAYERS, d, h), d ** -0.5),
        "b_fg": 3.0 + nrm(ks[9], (N_FOX_LAYERS, h), 0.5),
        "g_ffn_pre": 1.0 + nrm(ks[10], (DEPTH, d), 0.05),
        "g_ffn_post": 1.0 + nrm(ks[11], (DEPTH, d), 0.05),
        "w_ffn_gate": nrm(ks[12], (DEPTH, d, f), d ** -0.5),
        "w_ffn_up": nrm(ks[13], (DEPTH, d, f), d ** -0.5),
        "w_conv": nrm(ks[14], (DEPTH, CONV_WIDTH, f), CONV_WIDTH ** -0.5),
        "b_conv": nrm(ks[15], (DEPTH, f), 0.02),
        "w_ffn_down": nrm(ks[16], (DEPTH, f, d), f ** -0.5),
    }


def reference(x, c, w_mod, b_mod, g_mix_pre, g_mix_post, w_qkv, w_o, w_fg, b_fg,
              g_ffn_pre, g_ffn_post, w_ffn_gate, w_ffn_up, w_conv, b_conv, w_ffn_down):
    c_act = jax.nn.silu(c)
    for i in range(DEPTH):
        mod = jnp.einsum('bd,dm->bm', c_act, w_mod[i]) + b_mod[i]
        sh_a, sc_a, gt_a, sh_f, sc_f, gt_f = jnp.split(mod, N_MOD, axis=-1)

        h = modulate(rmsnorm(x, g_mix_pre[i]), sh_a, sc_a)
        qkv = jnp.einsum('bsd,de->bse', h, w_qkv[i])
        q, k, v = (split_heads(t) for t in jnp.split(qkv, 3, axis=-1))
        if i % N_MIXERS == 0:
            o = stick_breaking_attention(q, k, v)
        else:
            j = i // N_MIXERS
            f_logit = jnp.einsum('bsd,dh->bhs', h, w_fg[j]) + b_fg[j][None, :, None]
            log_f = jax.nn.log_sigmoid(f_logit.astype(jnp.float32))
            o = forgetting_attention(q, k, v, log_f)
        o = jnp.einsum('bse,ed->bsd', merge_heads(o), w_o[i])
        x = x + gt_a[:, None, :] * rmsnorm(o, g_mix_post[i])

        h = modulate(rmsnorm(x, g_ffn_pre[i]), sh_f, sc_f)
        gate = jnp.einsum('bsd,df->bsf', h, w_ffn_gate[i])
        up = jnp.einsum('bsd,df->bsf', h, w_ffn_up[i])
        gate = causal_depthwise_conv(gate, w_conv[i], b_conv[i])
        y = jnp.einsum('bsf,fd->bsd', jax.nn.silu(gate) * up, w_ffn_down[i])
        x = x + gt_f[:, None, :] * rmsnorm(y, g_ffn_post[i])
    return x
```

```python
import numpy as np
import ml_dtypes
from contextlib import ExitStack
import concourse.bass as bass
import concourse.mybir as mybir
from concourse.bass_utils import run_bass_kernel_spmd

F32 = mybir.dt.float32
BF16 = mybir.dt.bfloat16
AF = mybir.ActivationFunctionType
ALU = mybir.AluOpType
AX = mybir.AxisListType

D = 1024
NH = 16
DH = 64
FF = 2816
NFT = FF // 128
NDT = D // 128
EPS = 1e-6
QB = 512
NEG = -30000.0


class Sem:
    _n = 0

    def __init__(self, h):
        self.h = h
        Sem._n += 1
        self.key = Sem._n


class Prog:
    ENG = ['pe', 'act', 'dve', 'pool', 'sp']

    def __init__(self, nc, es):
        self.nc = nc
        self.eng = {'pe': nc.tensor, 'act': nc.scalar, 'dve': nc.vector, 'pool': nc.gpsimd, 'sp': nc.sync}
        self.sem = {n: Sem(es.enter_context(nc.semaphore("s_" + n))) for n in self.ENG}
        self.cnt = {n: 0 for n in self.ENG}
        self.seen = {n: {} for n in self.ENG}
        self.streams = []
        self.free_streams = {'sw': [], 'hw': []}
        self.es = es
        self.bar = Sem(es.enter_context(nc.semaphore("s_bar")))
        self.barcnt = 0
        self.nins = 0
        self._uid = 0

    def uid(self):
        self._uid += 1
        return self._uid

    def waits(self, qn, ws):
        seen = self.seen[qn]
        e = self.eng[qn]
        for w in ws:
            if w is None:
                continue
            s, v = w
            if v <= 0 or seen.get(s.key, 0) >= v:
                continue
            seen[s.key] = v
            e.wait_ge(s.h, v)
            self.nins += 1

    def op(self, qn, fn, ws=(), inc=True):
        self.waits(qn, ws)
        ins = fn(self.eng[qn])
        self.nins += 1
        if inc:
            ins.then_inc(self.sem[qn].h, 1)
            self.cnt[qn] += 1
            return (self.sem[qn], self.cnt[qn])
        return None

    def dma(self, qn, out, in_, st, ws=()):
        self.waits(qn, ws)
        st.cnt += 16
        self.eng[qn].dma_start(out=out, in_=in_).then_inc(st.sem.h, 16)
        self.nins += 1
        return (st.sem, st.cnt)

    def barrier(self):
        ws = [(self.sem[n], self.cnt[n]) for n in self.ENG]
        ws += [(st.sem, st.cnt) for st in self.streams if st.cnt > 0]
        self.waits('sp', ws)
        self.barcnt += 16
        self.eng['sp'].dma_start(out=self.bar_dst, in_=self.bar_src).then_inc(self.bar.h, 16)
        for n in self.ENG:
            self.waits(n, [(self.bar, self.barcnt)])

    def get_stream(self, kind):
        if self.free_streams[kind]:
            return self.free_streams[kind].pop()
        st = DmaStream(self, "d%s%d" % (kind, self.uid()))
        st.kind = kind
        return st


class DmaStream:
    def __init__(self, P, name):
        self.sem = Sem(P.es.enter_context(P.nc.semaphore(name)))
        self.cnt = 0
        P.streams.append(self)


class Buf:
    def __init__(self, P, es, t=None, name=None):
        self.P = P
        self.es = es
        self.t = t
        self.w = None
        self.r = {}
        self.name = name or ("b%d" % P.uid())
        self._ds = {}

    def ds(self, kind):
        if kind not in self._ds:
            self._ds[kind] = self.P.get_stream(kind)
        return self._ds[kind]


class Scope:
    def __init__(self, P, tag):
        self.P = P
        self.es = ExitStack()
        self.tag = tag
        self.bufs = []

    def sb(self, name, shape, dt):
        t = self.es.enter_context(self.P.nc.sbuf_tensor("%s_%s_%d" % (self.tag, name, self.P.uid()), shape, dt))
        b = Buf(self.P, self.es, t, name)
        self.bufs.append(b)
        return b

    def ps(self, name, shape, dt):
        t = self.es.enter_context(self.P.nc.psum_tensor("%s_%s_%d" % (self.tag, name, self.P.uid()), shape, dt))
        b = Buf(self.P, self.es, t, name)
        self.bufs.append(b)
        return b

    def close(self):
        self.P.barrier()
        for b in self.bufs:
            for kind, st in b._ds.items():
                self.P.free_streams[kind].append(st)
            b._ds = {}
        self.es.close()


def _rw(reads, writes):
    ws = []
    for b in reads:
        ws.append(b.w)
    for b in writes:
        ws.append(b.w)
        ws.extend(b.r.values())
    return ws


def _upd(tok, reads, writes):
    for b in reads:
        if b in writes:
            continue
        o = b.r.get(tok[0].key)
        if o is None or o[1] < tok[1]:
            b.r[tok[0].key] = tok
    for b in writes:
        b.w = tok
        b.r = {}


def wavefront(stage_lists, stride):
    sched = {}
    for i, st in enumerate(stage_lists):
        for k, fn in enumerate(st):
            sched.setdefault(stride * i + k, []).append(fn)
    for slot in sorted(sched):
        for fn in sched[slot]:
            fn()


def do(P, qn, fn, reads=(), writes=(), inc=True):
    ws = _rw(reads, writes)
    if qn == 'pe':
        pes = P.sem['pe']
        ws = [w for w in ws if w is not None and w[0] is not pes]
    tok = P.op(qn, fn, ws, inc=inc)
    if tok is not None:
        _upd(tok, reads, writes)
    return tok


def dma(P, qn, out_ap, in_ap, reads=(), writes=(), owner=None):
    ws = _rw(reads, writes)
    tok = P.dma(qn, out_ap, in_ap, owner.ds('sw' if qn == 'pool' else 'hw'), ws)
    _upd(tok, reads, writes)
    return tok


class AttnStream:
    def __init__(self, sc, S, fox, tag, nz, nC, no):
        self.S = S
        self.z = [sc.ps("z%s%d" % (tag, i), [128, 512], F32) for i in range(nz)]
        self.C = [sc.ps("C%s%d" % (tag, i), [128, 512], F32) for i in range(nC)] if not fox else []
        self.o = [sc.ps("o%s%d" % (tag, i), [128, 512], F32) for i in range(no)]
        if not fox:
            self.e = [sc.sb("e%s%d" % (tag, i), [128, 512], F32) for i in range(3)]
            self.sp = [sc.sb("sp%s%d" % (tag, i), [128, 512], BF16) for i in range(3)]
            self.E2 = [sc.sb("E2%s%d" % (tag, i), [128, 512], F32) for i in range(2)]
        else:
            self.rden = [sc.sb("rden%s%d" % (tag, i), [128, 512], F32) for i in range(2)]
            self.VO = [sc.sb("VO%s%d" % (tag, i), [128, S // 128, 128], BF16) for i in range(2)]
        self.A = [sc.sb("A%s%d" % (tag, i), [128, 512], BF16) for i in range(3)]
        self.qT = [sc.sb("qT%s%d" % (tag, i), [66, S], BF16) for i in range(2)]
        self.kT = [sc.sb("kT%s%d" % (tag, i), [66, S], BF16) for i in range(2)]
        self.oTq = [sc.sb("oTq%s%d" % (tag, i), [64, 512], BF16) for i in range(2)]


def emit_attention(P, streams, c, fox, load_head, store_q, v_of, bias_of=None):
    S = streams[0][0].S
    nqb = S // QB
    tpq = QB // 128
    msk = c['msk_fox'] if fox else c['msk_sb']
    KD = 66 if fox else 64

    class Ctx:
        pass

    ctxs = []
    for si, (AB, heads) in enumerate(streams):
        cx = Ctx()
        cx.AB = AB
        cx.si = si
        cx.heads = heads
        cx.steps = []
        for hi, h in enumerate(heads):
            for qb in range(nqb):
                jd = (qb + 1) * tpq - 1
                for j in range(jd, -1, -1):
                    m = j - qb * tpq
                    cx.steps.append(dict(hi=hi, h=h, qb=qb, j=j, c0=(128 * m if m >= 0 else 0), diag=(m >= 0),
                                         first=(j == jd), last=(j == 0), qbg=hi * nqb + qb))
        cx.n = len(cx.steps)
        cx.loaded = set()
        ctxs.append(cx)

    def maybe_load(cx, hi):
        if hi < len(cx.heads) and hi not in cx.loaded:
            cx.loaded.add(hi)
            load_head(cx.AB, cx.heads[hi], hi % 2)

    def QK(cx, s):
        AB = cx.AB
        st = cx.steps[s]
        sl = st['hi'] % 2
        z = AB.z[s % len(AB.z)]
        c0 = st['c0']
        q0 = st['qb'] * QB
        j = st['j']
        qT = AB.qT[sl]
        kT = AB.kT[sl]
        do(P, 'pe', lambda e: e.matmul(z.t[:, c0:QB], lhsT=kT.t[0:KD, j * 128:(j + 1) * 128],
                                       rhs=qT.t[0:KD, q0 + c0:q0 + QB],
                                       start=True, stop=True, skip_group_check=True),
           reads=[qT, kT] + ([c['ident'], msk] if st['diag'] else []), writes=[z], inc=not st['diag'])
        if st['diag']:
            do(P, 'pe', lambda e: e.matmul(z.t[:, c0:c0 + 128], lhsT=c['ident'].t[:], rhs=msk.t[:],
                                           start=False, stop=True, skip_group_check=True),
               reads=[qT, kT, c['ident'], msk], writes=[z])

    def PV(cx, s):
        AB = cx.AB
        st = cx.steps[s]
        c0 = st['c0']
        j = st['j']
        o = AB.o[st['qbg'] % len(AB.o)]
        A = AB.A[s % 3]
        vap, vbuf = v_of(AB, st['h'], st['hi'] % 2, j)
        if not fox:
            do(P, 'pe', lambda e: e.matmul(o.t[0:64, c0:QB], lhsT=vap, rhs=A.t[:, c0:QB],
                                           start=st['first'], stop=st['last'], skip_group_check=True),
               reads=[vbuf, A], writes=[o])
        else:
            do(P, 'pe', lambda e: e.matmul(o.t[:, c0:QB], lhsT=vap, rhs=A.t[:, c0:QB],
                                           start=st['first'], stop=st['last'], skip_group_check=True),
               reads=[vbuf, A], writes=[o])

    def EVAC(cx, s):
        AB = cx.AB
        st = cx.steps[s]
        o = AB.o[st['qbg'] % len(AB.o)]
        oq = AB.oTq[st['qbg'] % 2]
        if not fox:
            do(P, 'dve', lambda e: e.tensor_copy(out=oq.t[:], in_=o.t[0:64, :]), reads=[o], writes=[oq])
        else:
            rd = AB.rden[st['qbg'] % 2]
            do(P, 'dve', lambda e: e.reciprocal(out=rd.t[64:128, :], in_=o.t[64:128, :]), reads=[o], writes=[rd])
            do(P, 'dve', lambda e: e.tensor_tensor(out=oq.t[:], in0=o.t[0:64, :], in1=rd.t[64:128, :], op=ALU.mult),
               reads=[o, rd], writes=[oq])
        store_q(oq, st['h'], st['qb'])

    def tick(cx, t):
        AB = cx.AB
        steps = cx.steps
        n = cx.n
        if not fox:
            if 0 <= t - 1 < n and not steps[t - 1]['last']:
                s = t - 1
                st = steps[s]
                C = AB.C[st['qbg'] % len(AB.C)]
                sp = AB.sp[s % 3]
                c0 = st['c0']
                do(P, 'pe', lambda e: e.matmul(C.t[:, c0:QB], lhsT=c['Lcomp'].t[:], rhs=sp.t[:, c0:QB],
                                               start=False, stop=True, skip_group_check=True),
                   reads=[sp, c['Lcomp']], writes=[C])
            if 0 <= t < n:
                s = t
                st = steps[s]
                C = AB.C[st['qbg'] % len(AB.C)]
                sp = AB.sp[s % 3]
                c0 = st['c0']
                do(P, 'pe', lambda e: e.matmul(C.t[:, c0:QB], lhsT=c['Linc'].t[:], rhs=sp.t[:, c0:QB],
                                               start=st['first'], stop=True, skip_group_check=True),
                   reads=[sp, c['Linc']], writes=[C])
        if 0 <= t - 1 < n:
            PV(cx, t - 1)
        if 0 <= t + 2 < n:
            QK(cx, t + 2)
        if not fox:
            if 0 <= t + 1 < n:
                s = t + 1
                st = steps[s]
                z = AB.z[s % len(AB.z)]
                ee = AB.e[s % 3]
                c0 = st['c0']
                do(P, 'act', lambda e: e.activation(out=ee.t[:, c0:QB], in_=z.t[:, c0:QB], func=AF.Exp, scale=0.125),
                   reads=[z], writes=[ee])
            if 0 <= t < n:
                s = t
                st = steps[s]
                C = AB.C[st['qbg'] % len(AB.C)]
                E2 = AB.E2[s % 2]
                c0 = st['c0']
                do(P, 'act', lambda e: e.activation(out=E2.t[:, c0:QB], in_=C.t[:, c0:QB], func=AF.Exp, scale=-1.0),
                   reads=[C], writes=[E2])
            if 0 <= t + 1 < n:
                s = t + 1
                st = steps[s]
                ee = AB.e[s % 3]
                sp = AB.sp[s % 3]
                c0 = st['c0']
                do(P, 'act', lambda e: e.activation(out=sp.t[:, c0:QB], in_=ee.t[:, c0:QB], func=AF.Ln, bias=1.0, scale=1.0),
                   reads=[ee], writes=[sp])
        else:
            if 0 <= t + 1 < n:
                s = t + 1
                st = steps[s]
                z = AB.z[s % len(AB.z)]
                A = AB.A[s % 3]
                c0 = st['c0']
                bap, bbuf = bias_of(st['h'], st['qb'], st['j'])
                do(P, 'act', lambda e: e.activation(out=A.t[:, c0:QB], in_=z.t[:, c0:QB], func=AF.Exp, scale=0.125, bias=bap),
                   reads=[z, bbuf], writes=[A])
        if 0 <= t - 1 < n and steps[t - 1]['last']:
            EVAC(cx, t - 1)
        if not fox and 0 <= t < n:
            s = t
            st = steps[s]
            ee = AB.e[s % 3]
            E2 = AB.E2[s % 2]
            A = AB.A[s % 3]
            c0 = st['c0']
            do(P, 'dve', lambda e: e.tensor_tensor(out=A.t[:, c0:QB], in0=ee.t[:, c0:QB], in1=E2.t[:, c0:QB], op=ALU.mult),
               reads=[ee, E2], writes=[A])
        if 0 <= t < n and (t == 0 or steps[t]['hi'] != steps[t - 1]['hi']):
            maybe_load(cx, steps[t]['hi'] + 1)

    for cx in ctxs:
        maybe_load(cx, 0)
    nmax = max(cx.n for cx in ctxs)
    for t in range(-2, nmax + 2):
        for cx in ctxs:
            tick(cx, t)


CONST_NAMES = ['ident', 'msk_sb', 'msk_fox', 'Linc', 'Lcomp', 'U', 'ones']


def build_program(S, DEPTH, stop_after=None):
    NT = S // 128
    NB = S // QB
    NF = DEPTH // 2
    nc = bass.Bass("TRN2", target_bir_lowering=False)

    def din(name, shape, dt=F32):
        return nc.dram_tensor(name, shape, dt, kind="ExternalInput").ap()

    x_d = din("x", [S, D])
    c_d = din("c", [128, NDT])
    wmod_d = din("w_mod", [DEPTH, 128, NDT, 6 * D])
    wmodT_d = din("w_modT", [DEPTH, 128, 32, D])
    crow_d = din("c_row", [1, D])
    bmodc_d = din("b_mod_col", [DEPTH, 128, 48])
    bmod_d = din("b_mod", [DEPTH, 6 * D])
    gpre_d = din("g_pre", [DEPTH, 128, 2, NDT])
    gpost_d = din("g_post", [DEPTH, 2, D])
    wqkv_d = din("w_qkv", [DEPTH, 128, NDT, 3 * D])
    wo_d = din("w_o", [DEPTH, 128, NDT, D])
    wfg_d = din("w_fg", [max(NF, 1), 128, NDT, NH])
    bfg_d = din("b_fg", [max(NF, 1), NH])
    wg_d = din("w_g", [DEPTH, NFT, 128, NDT, 128])
    wu_d = din("w_u", [DEPTH, NFT, 128, NDT, 128])
    wd_d = din("w_d", [DEPTH, 128, NFT, D])
    wcv_d = din("w_conv", [DEPTH, 128, NFT, 3])
    bcv_d = din("b_conv", [DEPTH, 128, NFT])
    cst_d = din("consts", [len(CONST_NAMES), 128, 128])
    y_d = nc.dram_tensor("y", [S, D], F32, kind="ExternalOutput").ap()
    qT_d = nc.dram_tensor("qT_s", [D, S], BF16, kind="Internal").ap()
    kT_d = nc.dram_tensor("kT_s", [D, S], BF16, kind="Internal").ap()
    oT_d = nc.dram_tensor("oT_s", [D, S], BF16, kind="Internal").ap()
    aug_d = nc.dram_tensor("aug_s", [NH, 2, S], BF16, kind="Internal").ap()
    gg_d = nc.dram_tensor("gg_s", [DEPTH, 2, 128, D], F32, kind="Internal").ap()
    bar_d = nc.dram_tensor("bar_s", [2, 16], F32, kind="Internal").ap()
    wqkvb_d = nc.dram_tensor("wqkvb_s", [128, NDT, 3 * D], BF16, kind="Internal").ap()
    wob_d = nc.dram_tensor("wob_s", [128, NDT, D], BF16, kind="Internal").ap()
    wgb_d = nc.dram_tensor("wgb_s", [NFT, 128, NDT, 128], BF16, kind="Internal").ap()
    wub_d = nc.dram_tensor("wub_s", [NFT, 128, NDT, 128], BF16, kind="Internal").ap()
    wdb_d = nc.dram_tensor("wdb_s", [128, NFT, D], BF16, kind="Internal").ap()

    es = ExitStack()
    with es:
        P = Prog(nc, es)
        P.bar_dst = bar_d[0:1, :]
        P.bar_src = cst_d[0, 0:1, 0:16]
        G = Scope(P, "g")
        c = {}
        for i, nm in enumerate(['ident', 'msk_sb', 'msk_fox', 'Linc', 'Lcomp']):
            c[nm] = G.sb("c_" + nm, [128, 128], BF16)
            dma(P, 'pool', c[nm].t[:], cst_d[i], writes=[c[nm]], owner=c[nm])
        for i, nm in [(0, 'ident_f'), (5, 'U_f'), (6, 'ones_f')]:
            c[nm] = G.sb("c_" + nm, [128, 128], F32)
            dma(P, 'sp', c[nm].t[:], cst_d[i], writes=[c[nm]], owner=c[nm])
        c['ones64'] = G.sb("ones64", [128, 64], BF16)
        do(P, 'dve', lambda e: e.memset(c['ones64'].t[:], 1.0), writes=[c['ones64']])
        neghalf = G.sb("neghalf", [128, 1], F32)
        do(P, 'dve', lambda e: e.memset(neghalf.t[:], -0.5), writes=[neghalf])
        wq_b = Buf(P, None, None, "wq_b")
        wo_b = Buf(P, None, None, "wo_b")
        wg_b = Buf(P, None, None, "wg_b")
        wu_b = Buf(P, None, None, "wu_b")
        wd_b = Buf(P, None, None, "wd_b")
        G.bufs += [wq_b, wo_b, wg_b, wu_b, wd_b]

        def cast_wqkv(l_):
            for dt in range(NDT):
                dma(P, 'pool', wqkvb_d[:, dt, :], wqkv_d[l_, :, dt, :], writes=[wq_b], owner=wq_b)

        def cast_ffn(l_):
            dma(P, 'pool', wob_d, wo_d[l_], writes=[wo_b], owner=wo_b)
            for q4 in range(0, NFT, 2):
                dma(P, 'pool', wdb_d[:, q4:q4 + 2, :], wd_d[l_, :, q4:q4 + 2, :], writes=[wd_b], owner=wd_b)
            for q4 in range(0, NFT, 2):
                dma(P, 'pool', wgb_d[q4:q4 + 2], wg_d[l_, q4:q4 + 2], writes=[wg_b], owner=wg_b)
                dma(P, 'pool', wub_d[q4:q4 + 2], wu_d[l_, q4:q4 + 2], writes=[wu_b], owner=wu_b)

        cast_wqkv(0)
        abcol = G.sb("abcol", [128, DEPTH, 4, NDT], F32)
        stat = [G.sb("stat%d" % i, [128, 4], F32) for i in range(8)]
        statn = [0]

        def rms_stats(src_ap, srcbuf, junk, width_scale):
            sb_ = stat[statn[0] % len(stat)]
            statn[0] += 1
            do(P, 'act', lambda e: e.activation(out=junk.t[:], in_=src_ap, func=AF.Square, accum_out=sb_.t[:, 0:1]),
               reads=[srcbuf], writes=[junk, sb_])
            do(P, 'act', lambda e: e.activation(out=sb_.t[:, 1:2], in_=sb_.t[:, 0:1], func=AF.Ln, scale=width_scale, bias=EPS),
               reads=[sb_], writes=[sb_])
            do(P, 'act', lambda e: e.activation(out=sb_.t[:, 2:3], in_=sb_.t[:, 1:2], func=AF.Exp, scale=-0.5),
               reads=[sb_], writes=[sb_])
            return sb_.t[:, 2:3], sb_

        PR = Scope(P, "pr")
        cT = PR.sb("cT", [128, NDT], F32)
        cact = PR.sb("cact", [128, NDT], F32)
        crep = PR.sb("crep", [128, NDT, 128], F32)
        dma(P, 'sp', cT.t[:], c_d, writes=[cT], owner=cT)
        do(P, 'act', lambda e: e.activation(out=cact.t[:], in_=cT.t[:], func=AF.Silu), reads=[cT], writes=[cact])
        for dt in range(NDT):
            do(P, 'dve', lambda e, dt=dt: e.tensor_scalar(out=crep.t[:, dt, :], in0=c['ones_f'].t[:], scalar1=cact.t[:, dt:dt + 1],
                                                          scalar2=None, op0=ALU.mult), reads=[cact, c['ones_f']], writes=[crep])
        wm = [PR.sb("wm%d" % i, [128, NDT, 512], F32) for i in range(2)]
        wmT = [PR.sb("wmT%d" % i, [128, 4, D], F32) for i in range(3)]
        cbc = PR.sb("cbc", [128, D], F32)
        junkf = PR.sb("junkf", [128, D], F32)
        dma(P, 'sp', cbc.t[:], crow_d[0, :].partition_broadcast(128), writes=[cbc], owner=cbc)
        do(P, 'act', lambda e: e.activation(out=cbc.t[:], in_=cbc.t[:], func=AF.Silu), reads=[cbc], writes=[cbc])
        modcol = PR.sb("modcol", [128, 48], F32)
        nchT = 0
        ggps = [PR.ps("ggps%d" % i, [128, 512], F32) for i in range(2)]
        bmodc = PR.sb("bmodc", [128, 48], F32)
        modc = PR.sb("modc", [128, 48], F32)
        gpre = PR.sb("gpre", [128, 2, NDT], F32)
        bmbc = PR.sb("bmbc", [128, 2, D], F32)
        gpbc = PR.sb("gpbc", [128, 2, D], F32)
        ggst = [PR.sb("ggst%d" % i, [128, 512], F32) for i in range(2)]
        nchunk = 0
        ngg = 0
        for l in range(DEPTH):
            dma(P, 'sp', bmodc.t[:], bmodc_d[l], writes=[bmodc], owner=bmodc)
            dma(P, 'sp', gpre.t[:], gpre_d[l], writes=[gpre], owner=gpre)
            for which, v in enumerate((2, 5)):
                dma(P, 'sp', bmbc.t[:, which, :], bmod_d[l, v * D:(v + 1) * D].partition_broadcast(128),
                    writes=[bmbc], owner=bmbc)
                dma(P, 'sp', gpbc.t[:, which, :], gpost_d[l, which, :].partition_broadcast(128),
                    writes=[gpbc], owner=gpbc)
            for mcb in range(12):
                v = mcb // 2
                half = mcb % 2
                if v in (2, 5):
                    w_ = wm[nchunk % 2]
                    nchunk += 1
                    dma(P, 'pool', w_.t[:], wmod_d[l, :, :, mcb * 512:(mcb + 1) * 512], writes=[w_], owner=w_)
                    which = 0 if v == 2 else 1
                    gp = ggps[ngg % 2]
                    gs = ggst[ngg % 2]
                    ngg += 1
                    for dt in range(NDT):
                        do(P, 'pe', lambda e, dt=dt, gp=gp, w_=w_: e.matmul(gp.t[:], lhsT=crep.t[:, dt, :], rhs=w_.t[:, dt, :],
                                                                            start=(dt == 0), stop=(dt == NDT - 1)),
                           reads=[crep, w_], writes=[gp], inc=(dt == NDT - 1))
                    do(P, 'dve', lambda e, gp=gp, gs=gs, which=which, half=half: e.tensor_tensor(
                        out=gs.t[:], in0=gp.t[:], in1=bmbc.t[:, which, half * 512:(half + 1) * 512], op=ALU.add),
                       reads=[gp, bmbc], writes=[gs])
                    do(P, 'dve', lambda e, gs=gs, which=which, half=half: e.tensor_tensor(
                        out=gs.t[:], in0=gs.t[:], in1=gpbc.t[:, which, half * 512:(half + 1) * 512], op=ALU.mult),
                       reads=[gs, gpbc], writes=[gs])
                    dma(P, 'sp', gg_d[l, which, :, half * 512:(half + 1) * 512], gs.t[:], reads=[gs], owner=gs)
                else:
                    kq = mcb * 4 if mcb < 4 else mcb * 4 - 8
                    wt = wmT[nchT % 3]
                    nchT += 1
                    dma(P, 'sp', wt.t[:], wmodT_d[l, :, kq:kq + 4, :], writes=[wt], owner=wt)
                    for sc_ in range(4):
                        mc = mcb * 4 + sc_
                        do(P, 'dve', lambda e, mc=mc, sc_=sc_, wt=wt: e.scalar_tensor_tensor(
                            out=junkf.t[:], in0=wt.t[:, sc_, :], scalar=1.0, in1=cbc.t[:], op0=ALU.mult, op1=ALU.mult,
                            accum_out=modcol.t[:, mc:mc + 1]), reads=[wt, cbc], writes=[junkf, modcol])
            for (a0, a1) in ((0, 16), (24, 40)):
                do(P, 'dve', lambda e, a0=a0, a1=a1: e.tensor_tensor(out=modc.t[:, a0:a1], in0=modcol.t[:, a0:a1],
                                                                     in1=bmodc.t[:, a0:a1], op=ALU.add),
                   reads=[modcol, bmodc], writes=[modc])
            for which, (sh, scl) in enumerate(((0, 1), (3, 4))):
                do(P, 'dve', lambda e, which=which, scl=scl, l=l: e.scalar_tensor_tensor(
                    out=abcol.t[:, l, 2 * which, :], in0=modc.t[:, scl * 8:(scl + 1) * 8], scalar=1.0, in1=gpre.t[:, which, :],
                    op0=ALU.add, op1=ALU.mult), reads=[modc, gpre], writes=[abcol])
                do(P, 'dve', lambda e, which=which, sh=sh, l=l: e.tensor_copy(
                    out=abcol.t[:, l, 2 * which + 1, :], in_=modc.t[:, sh * 8:(sh + 1) * 8]), reads=[modc], writes=[abcol])
        PR.close()
        if stop_after == ('P',):
            G.close()
            return nc, P

        def norm_transpose(sc, src_ap, srcbuf, xn, junk, tp, hT, col0, l, which):
            rstd, sb_ = rms_stats(src_ap, srcbuf, junk, 1.0 / D)
            do(P, 'dve', lambda e: e.tensor_scalar(out=xn.t[:], in0=src_ap, scalar1=rstd, scalar2=None, op0=ALU.mult),
               reads=[srcbuf, sb_], writes=[xn])
            for dt in range(NDT):
                do(P, 'pe', lambda e, dt=dt: e.transpose(out=tp.t[:, dt, :], in_=xn.t[:, dt * 128:(dt + 1) * 128],
                                                         identity=c['ident'].t[:]),
                   reads=[xn, c['ident']], writes=[tp], inc=(dt == NDT - 1))
            for dt in range(NDT):
                do(P, 'dve', lambda e, dt=dt: e.tensor_scalar(out=hT.t[:, dt, col0:col0 + 128], in0=tp.t[:, dt, :],
                                                              scalar1=abcol.t[:, l, 2 * which, dt:dt + 1],
                                                              scalar2=abcol.t[:, l, 2 * which + 1, dt:dt + 1],
                                                              op0=ALU.mult, op1=ALU.add),
                   reads=[tp, abcol], writes=[hT])

        for l in range(DEPTH):
            fox = (l % 2 == 1)
            fl = l // 2
            xsrc = x_d if l == 0 else y_d
            SAB = Scope(P, "ab%d" % l)
            Vres = SAB.sb("Vres", [128, NT, D], BF16)
            if fox:
                nlf = SAB.sb("nlf", [128, NT, NH], F32)
                tab = SAB.sb("tab", [128, NB, NT, NH], F32)
            SA = Scope(P, "a%d" % l)
            Wqkv = SA.sb("Wqkv", [128, NDT, 3 * D], BF16)
            for dt in range(0, NDT, 2):
                dma(P, 'sp', Wqkv.t[:, dt:dt + 2, :], wqkvb_d[:, dt:dt + 2, :], reads=[wq_b], writes=[Wqkv], owner=Wqkv)
            if fox:
                wfg = SA.sb("wfg", [128, NDT, NH], BF16)
                dma(P, 'pool', wfg.t[:], wfg_d[fl], writes=[wfg], owner=wfg)
                bfg = SA.sb("bfg", [128, NH], F32)
                dma(P, 'sp', bfg.t[:], bfg_d[fl].partition_broadcast(128), writes=[bfg], owner=bfg)
                fsb = [SA.sb("fsb%d" % i, [128, NH], F32) for i in range(2)]
            hT = [SA.sb("hT%d" % i, [128, NDT, 512], BF16) for i in range(2)]
            xt = [SA.sb("xt%d" % i, [128, D], F32) for i in range(3)]
            xn = [SA.sb("xn%d" % i, [128, D], BF16) for i in range(2)]
            junk = SA.sb("junk", [128, D], BF16)
            qks = [SA.sb("qks%d" % i, [128, 4, 512], BF16) for i in range(2)]
            SAP = Scope(P, "ap%d" % l)
            tp = [SAP.ps("tp%d" % i, [128, NDT, 128], BF16) for i in range(2)]
            mm = [SAP.ps("mm%d" % i, [128, 512], F32) for i in range(4)]
            nmm = 0
            nqk = 0
            if fox:
                flps = SAP.ps("flps", [128, NH], F32)
            def A_stages(tb, i):
                j = tb * 4 + i
                h_ = hT[tb % 2]
                x_ = xt[j % 3]
                xn_ = xn[j % 2]
                tp_ = tp[j % 2]
                stt = {}

                def s0():
                    dma(P, 'sp', x_.t[:], xsrc[j * 128:(j + 1) * 128, :], writes=[x_], owner=x_)
                    sb_ = stat[statn[0] % len(stat)]
                    statn[0] += 1
                    stt['sb'] = sb_
                    do(P, 'act', lambda e: e.activation(out=junk.t[:], in_=x_.t[:], func=AF.Square, accum_out=sb_.t[:, 0:1]),
                       reads=[x_], writes=[junk, sb_])

                def s1():
                    sb_ = stt['sb']
                    do(P, 'act', lambda e: e.activation(out=sb_.t[:, 1:2], in_=sb_.t[:, 0:1], func=AF.Ln, scale=1.0 / D, bias=EPS),
                       reads=[sb_], writes=[sb_])

                def s2():
                    sb_ = stt['sb']
                    do(P, 'act', lambda e: e.activation(out=sb_.t[:, 2:3], in_=sb_.t[:, 1:2], func=AF.Exp, scale=-0.5),
                       reads=[sb_], writes=[sb_])

                def s3():
                    sb_ = stt['sb']
                    do(P, 'dve', lambda e: e.tensor_scalar(out=xn_.t[:], in0=x_.t[:], scalar1=sb_.t[:, 2:3], scalar2=None, op0=ALU.mult),
                       reads=[x_, sb_], writes=[xn_])

                def s4():
                    for dt in range(NDT):
                        do(P, 'pe', lambda e, dt=dt: e.transpose(out=tp_.t[:, dt, :], in_=xn_.t[:, dt * 128:(dt + 1) * 128],
                                                                 identity=c['ident'].t[:]),
                           reads=[xn_, c['ident']], writes=[tp_], inc=(dt == NDT - 1))

                def s5():
                    for dt in range(NDT):
                        do(P, 'dve', lambda e, dt=dt: e.tensor_scalar(out=h_.t[:, dt, i * 128:(i + 1) * 128], in0=tp_.t[:, dt, :],
                                                                      scalar1=abcol.t[:, l, 0, dt:dt + 1],
                                                                      scalar2=abcol.t[:, l, 1, dt:dt + 1],
                                                                      op0=ALU.mult, op1=ALU.add),
                           reads=[tp_, abcol], writes=[h_])
                return [s0, s1, s2, s3, s4, s5]

            cntA = dict(mm=0, qk=0)

            def A_groups(tb):
                h_ = hT[tb % 2]
                groups = []
                for g in range(4):
                    for u in range(4):
                        def grp(g=g, u=u):
                            if u == 0:
                                cntA['qs'] = qks[cntA['qk'] % 2]
                                cntA['qk'] += 1
                            qs = cntA['qs']
                            et = g * 4 + u
                            m_ = mm[cntA['mm'] % 4]
                            cntA['mm'] += 1
                            for dt in range(NDT):
                                do(P, 'pe', lambda e, dt=dt: e.matmul(
                                    m_.t[:], lhsT=Wqkv.t[:, dt, et * 128:(et + 1) * 128], rhs=h_.t[:, dt, :],
                                    start=(dt == 0), stop=(dt == NDT - 1)),
                                   reads=[Wqkv, h_], writes=[m_], inc=(dt == NDT - 1))
                            do(P, 'act', lambda e: e.activation(out=qs.t[:, u, :], in_=m_.t[:], func=AF.Copy),
                               reads=[m_], writes=[qs])
                            if u == 3:
                                dst = (qT_d if g < 2 else kT_d)[(g % 2) * 512:(g % 2) * 512 + 512, tb * 512:(tb + 1) * 512]
                                dma(P, 'sp', dst.rearrange("(u p) t -> p u t", p=128), qs.t[:], reads=[qs], owner=qs)
                        groups.append(grp)
                for i in range(4):
                    j = tb * 4 + i
                    for hf in range(2):
                        def grp(i=i, j=j, hf=hf):
                            m_ = mm[cntA['mm'] % 4]
                            cntA['mm'] += 1
                            for dt in range(NDT):
                                do(P, 'pe', lambda e, dt=dt: e.matmul(
                                    m_.t[:], lhsT=h_.t[:, dt, i * 128:(i + 1) * 128],
                                    rhs=Wqkv.t[:, dt, 2 * D + hf * 512:2 * D + (hf + 1) * 512],
                                    start=(dt == 0), stop=(dt == NDT - 1)),
                                   reads=[Wqkv, h_], writes=[m_], inc=(dt == NDT - 1))
                            do(P, 'dve', lambda e: e.tensor_copy(out=Vres.t[:, j, hf * 512:(hf + 1) * 512], in_=m_.t[:]),
                               reads=[m_], writes=[Vres])
                            if fox and hf == 1:
                                for dt in range(NDT):
                                    do(P, 'pe', lambda e, dt=dt: e.matmul(flps.t[:], lhsT=h_.t[:, dt, i * 128:(i + 1) * 128],
                                                                          rhs=wfg.t[:, dt, :], start=(dt == 0), stop=(dt == NDT - 1)),
                                       reads=[wfg, h_], writes=[flps], inc=(dt == NDT - 1))
                                f_ = fsb[j % 2]
                                do(P, 'dve', lambda e: e.tensor_tensor(out=f_.t[:], in0=flps.t[:], in1=bfg.t[:], op=ALU.add),
                                   reads=[flps, bfg], writes=[f_])
                                do(P, 'act', lambda e: e.activation(out=f_.t[:], in_=f_.t[:], func=AF.Exp, scale=-1.0),
                                   reads=[f_], writes=[f_])
                                do(P, 'act', lambda e: e.activation(out=nlf.t[:, j, :], in_=f_.t[:], func=AF.Ln, bias=1.0),
                                   reads=[f_], writes=[nlf])
                        groups.append(grp)
                return groups

            wavefront([A_stages(0, i) for i in range(4)], 2)
            for tb in range(NB):
                sched = {}
                if tb + 1 < NB:
                    for i_ in range(4):
                        for k2, fn_ in enumerate(A_stages(tb + 1, i_)):
                            sched.setdefault(4 * i_ + k2, []).append(fn_)
                for gi, grp in enumerate(A_groups(tb)):
                    grp()
                    for fn_ in sched.pop(gi, []):
                        fn_()
                assert not sched
            SAP.close()
            if fox:
                cps = SA.ps("cps", [128, NT * NH], F32)
                tps = SA.ps("tps", [128, NT * NH], F32)
                cumT = SA.sb("cumT", [128, NT, NH], F32)
                crefs = SA.sb("crefs", [128, NB, NH], F32)
                totT = SA.sb("totT", [128, NH, NT], F32)
                incl = SA.sb("incl", [128, NH, NT], F32)
                smask = SA.sb("smask", [128, NH, NT], F32)
                for j in range(NT):
                    do(P, 'pe', lambda e, j=j: e.matmul(cps.t[:, j * NH:(j + 1) * NH], lhsT=c['U_f'].t[:], rhs=nlf.t[:, j, :],
                                                        start=True, stop=True, skip_group_check=True),
                       reads=[nlf, c['U_f']], writes=[cps], inc=(j == NT - 1))
                for j in range(NT):
                    do(P, 'pe', lambda e, j=j: e.matmul(tps.t[:, j * NH:(j + 1) * NH], lhsT=c['ones_f'].t[:], rhs=nlf.t[:, j, :],
                                                        start=True, stop=True, skip_group_check=True),
                       reads=[nlf, c['ones_f']], writes=[tps], inc=(j == NT - 1))
                do(P, 'dve', lambda e: e.memset(smask.t[:], 1.0), writes=[smask])
                do(P, 'dve', lambda e: e.memset(smask.t[:, :, 0:1], 0.0), writes=[smask])
                do(P, 'dve', lambda e: e.tensor_copy(out=totT.t[:], in_=tps.t[:].rearrange("p (j h) -> p h j", h=NH)),
                   reads=[tps], writes=[totT])
                do(P, 'dve', lambda e: e.tensor_tensor_scan(out=incl.t[:].rearrange("p h j -> p (h j)"),
                                                            data0=smask.t[:].rearrange("p h j -> p (h j)"),
                                                            data1=totT.t[:].rearrange("p h j -> p (h j)"),
                                                            initial=0.0, op0=ALU.mult, op1=ALU.add),
                   reads=[smask, totT], writes=[incl])
                do(P, 'dve', lambda e: e.tensor_copy(out=crefs.t[:].rearrange("p q h -> p h q"),
                                                     in_=incl.t[:].rearrange("p h (q f) -> p h q f", f=4)[:, :, :, 3]),
                   reads=[incl], writes=[crefs])
                do(P, 'dve', lambda e: e.tensor_tensor(out=totT.t[:], in0=incl.t[:], in1=totT.t[:], op=ALU.subtract),
                   reads=[incl, totT], writes=[totT])
                do(P, 'dve', lambda e: e.tensor_tensor(out=cumT.t[:].rearrange("p j h -> p h j"),
                                                       in0=cps.t[:].rearrange("p (j h) -> p h j", h=NH), in1=totT.t[:], op=ALU.add),
                   reads=[cps, totT], writes=[cumT])
                for qb in range(NB):
                    nj = 4 * qb + 4
                    do(P, 'dve', lambda e, qb=qb, nj=nj: e.tensor_tensor(
                        out=tab.t[:, qb, 0:nj, :], in0=cumT.t[:, 0:nj, :],
                        in1=crefs.t[:, qb:qb + 1, :].to_broadcast([128, nj, NH]), op=ALU.subtract),
                       reads=[cumT, crefs], writes=[tab])
                HB = max(NB // 2, 1)
                HW = HB * 512
                trp = SA.ps("trp", [NH, HW], F32)
                dd = SA.sb("dd", [NH, HW], F32)
                dcol = SA.sb("dcol", [NH, HB], F32)
                dhi = [SA.sb("dhi%d" % i, [NH, HW], BF16) for i in range(1)]
                dlo = [SA.sb("dlo%d" % i, [NH, HW], BF16) for i in range(1)]
                for hf in range(NB // HB):
                    for i in range(HB * 4):
                        do(P, 'pe', lambda e, i=i, hf=hf: e.transpose(out=trp.t[:, i * 128:(i + 1) * 128],
                                                                      in_=cumT.t[:, hf * HB * 4 + i, :], identity=c['ident_f'].t[:]),
                           reads=[cumT, c['ident_f']], writes=[trp], inc=(i == HB * 4 - 1))
                    hi_ = dhi[0]
                    lo_ = dlo[0]
                    do(P, 'dve', lambda e: e.tensor_copy(out=dcol.t[:], in_=trp.t[:].rearrange("h (q f) -> h q f", f=512)[:, :, 511]),
                       reads=[trp], writes=[dcol])
                    do(P, 'dve', lambda e: e.tensor_tensor(out=dd.t[:].rearrange("h (q f) -> h q f", f=512),
                                                           in0=trp.t[:].rearrange("h (q f) -> h q f", f=512),
                                                           in1=dcol.t[:].unsqueeze(2).to_broadcast([NH, HB, 512]), op=ALU.subtract),
                       reads=[trp, dcol], writes=[dd])
                    do(P, 'dve', lambda e: e.tensor_scalar(out=dd.t[:], in0=dd.t[:], scalar1=-8.0, scalar2=None, op0=ALU.mult),
                       reads=[dd], writes=[dd])
                    do(P, 'dve', lambda e, hi_=hi_: e.tensor_copy(out=hi_.t[:], in_=dd.t[:]), reads=[dd], writes=[hi_])
                    do(P, 'dve', lambda e, hi_=hi_, lo_=lo_: e.tensor_tensor(out=lo_.t[:], in0=dd.t[:], in1=hi_.t[:], op=ALU.subtract),
                       reads=[dd, hi_], writes=[lo_])
                    dma(P, 'sp', aug_d[:, 0, hf * HW:(hf + 1) * HW], hi_.t[:], reads=[hi_], owner=hi_)
                    dma(P, 'sp', aug_d[:, 1, hf * HW:(hf + 1) * HW], lo_.t[:], reads=[lo_], owner=lo_)
            SA.close()
            if stop_after == ('A', l):
                SAB.close()
                break

            SB_ = Scope(P, "b%d" % l)
            cast_ffn(l)
            if fox:
                strm = [(AttnStream(SB_, S, True, "a", 3, 0, 2), list(range(NH)))]
            else:
                strm = [(AttnStream(SB_, S, False, "a", 2, 1, 1), list(range(0, NH, 2))),
                        (AttnStream(SB_, S, False, "b", 2, 1, 1), list(range(1, NH, 2)))]
            if fox:
                for AB, _h in strm:
                    for sl in range(2):
                        do(P, 'dve', lambda e, sl=sl, AB=AB: e.memset(AB.kT[sl].t[64:66, :], 1.0), writes=[AB.kT[sl]])
                        do(P, 'pool', lambda e, sl=sl, AB=AB: e.memset(AB.VO[sl].t[:, :, 64:128], 1.0), writes=[AB.VO[sl]])

            def load_head(AB, h, sl, fox=fox, Vres=Vres):
                dma(P, 'sp', AB.qT[sl].t[0:64, :], qT_d[h * 64:(h + 1) * 64, :], writes=[AB.qT[sl]], owner=AB.qT[sl])
                dma(P, 'sp', AB.kT[sl].t[0:64, :], kT_d[h * 64:(h + 1) * 64, :], writes=[AB.kT[sl]], owner=AB.kT[sl])
                if fox:
                    dma(P, 'sp', AB.qT[sl].t[64:66, :], aug_d[h], writes=[AB.qT[sl]], owner=AB.qT[sl])
                    do(P, 'pool', lambda e: e.tensor_copy(out=AB.VO[sl].t[:, :, 0:64], in_=Vres.t[:, :, h * 64:(h + 1) * 64]),
                       reads=[Vres], writes=[AB.VO[sl]])

            def store_q(oq, h, qb):
                dma(P, 'sp', oT_d[h * 64:(h + 1) * 64, qb * QB:(qb + 1) * QB], oq.t[:], reads=[oq], owner=oq)

            def v_of(AB, h, sl, j, Vres=Vres, fox=fox):
                if fox:
                    return AB.VO[sl].t[:, j, :], AB.VO[sl]
                return Vres.t[:, j, h * 64:(h + 1) * 64], Vres

            bias_of = None
            if fox:
                def bias_of(h, qb, j, tab=tab):
                    return tab.t[:, qb, j, h:h + 1], tab
            emit_attention(P, strm, c, fox, load_head, store_q, v_of, bias_of)
            if l + 1 < DEPTH:
                cast_wqkv(l + 1)
            SB_.close()
            SAB.close()
            if stop_after == ('B', l):
                break

            SC = Scope(P, "c%d" % l)
            Wo = SC.sb("Wo", [128, NDT, D], BF16)
            dma(P, 'sp', Wo.t[:], wob_d, reads=[wo_b], writes=[Wo], owner=Wo)
            Wd = SC.sb("Wd", [128, NFT, D], BF16)
            for q4 in range(0, NFT, 11):
                dma(P, 'sp', Wd.t[:, q4:q4 + 11, :], wdb_d[:, q4:q4 + 11, :], reads=[wd_b], writes=[Wd], owner=Wd)
            wcv = SC.sb("wcv", [128, NFT, 3], F32)
            bcv = SC.sb("bcv", [128, NFT], F32)
            dma(P, 'sp', wcv.t[:], wcv_d[l], writes=[wcv], owner=wcv)
            dma(P, 'sp', bcv.t[:], bcv_d[l], writes=[bcv], owner=bcv)
            GG = SC.sb("GG", [128, 2, D], F32)
            for which in range(2):
                dma(P, 'sp', GG.t[:, which, :], gg_d[l, which], writes=[GG], owner=GG)
            halo = SC.sb("halo", [128, NFT, 2], F32)
            do(P, 'dve', lambda e: e.memset(halo.t[:], 0.0), writes=[halo])
            oTb = SC.sb("oTb", [128, NDT, 512], BF16)
            xt = [SC.sb("xt%d" % i, [128, D], F32) for i in range(3)]
            xnw = [SC.sb("xnw%d" % i, [128, D], F32) for i in range(3)]
            xn = [SC.sb("xn%d" % i, [128, D], BF16) for i in range(2)]
            junk = SC.sb("junk", [128, D], BF16)
            t1 = [SC.sb("t1%d" % i, [128, D], F32) for i in range(2)]
            xo = [SC.sb("xo%d" % i, [128, D], F32) for i in range(2)]
            h2T = [SC.sb("h2T%d" % i, [128, NDT, 512], BF16) for i in range(2)]
            aT = SC.sb("aT", [128, NFT, 512], BF16)
            wgr = [SC.sb("wg%d" % i, [128, NDT, 128], BF16) for i in range(4)]
            wur = [SC.sb("wu%d" % i, [128, NDT, 128], BF16) for i in range(4)]
            gbuf = [SC.sb("gbuf%d" % i, [128, 514], F32) for i in range(2)]
            cv = [SC.sb("cv%d" % i, [128, 512], F32) for i in range(3)]
            sg = [SC.sb("sg%d" % i, [128, 512], F32) for i in range(3)]
            tp = SC.ps("tp", [128, NDT, 128], BF16)
            yps = [SC.ps("yps%d" % i, [128, D], F32) for i in range(2)]
            gps = [SC.ps("gps%d" % i, [128, 512], F32) for i in range(2)]
            ups = SC.ps("ups", [128, 512], F32)
            ydr = [Buf(P, None, None, "ydr%d" % j) for j in range(NT)]
            cnt = dict(y=0, t1=0, xt=0, xnw=0, xn=0, xo=0)
            Tstate = {}

            def T_stages(tb, i):
                j = tb * 4 + i
                hT_ = h2T[tb % 2]
                stt = {}

                def s0():
                    y_ = yps[cnt['y'] % 2]
                    cnt['y'] += 1
                    stt['y'] = y_
                    for hf in range(2):
                        for et in range(NDT):
                            do(P, 'pe', lambda e, et=et, hf=hf: e.matmul(
                                y_.t[:, hf * 512:(hf + 1) * 512], lhsT=oTb.t[:, et, i * 128:(i + 1) * 128],
                                rhs=Wo.t[:, et, hf * 512:(hf + 1) * 512], start=(et == 0), stop=(et == NDT - 1)),
                               reads=[oTb, Wo], writes=[y_], inc=(et == NDT - 1 and hf == 1))
                    x_ = xt[cnt['xt'] % 3]
                    cnt['xt'] += 1
                    stt['x'] = x_
                    dma(P, 'pool', x_.t[:], xsrc[j * 128:(j + 1) * 128, :], reads=[ydr[j]] if l > 0 else [], writes=[x_], owner=x_)

                def mk_stats(key_src, key_out):
                    def a():
                        src = stt[key_src]
                        sb_ = stat[statn[0] % len(stat)]
                        statn[0] += 1
                        stt[key_out] = sb_
                        do(P, 'act', lambda e: e.activation(out=junk.t[:], in_=src.t[:], func=AF.Square, accum_out=sb_.t[:, 0:1]),
                           reads=[src], writes=[junk, sb_])

                    def b():
                        sb_ = stt[key_out]
                        do(P, 'act', lambda e: e.activation(out=sb_.t[:, 1:2], in_=sb_.t[:, 0:1], func=AF.Ln, scale=1.0 / D, bias=EPS),
                           reads=[sb_], writes=[sb_])

                    def c_():
                        sb_ = stt[key_out]
                        do(P, 'act', lambda e: e.activation(out=sb_.t[:, 2:3], in_=sb_.t[:, 1:2], func=AF.Exp, scale=-0.5),
                           reads=[sb_], writes=[sb_])
                    return [a, b, c_]

                def s4():
                    t_ = t1[cnt['t1'] % 2]
                    cnt['t1'] += 1
                    stt['t'] = t_
                    y_ = stt['y']
                    sb_ = stt['st1']
                    do(P, 'act', lambda e: e.activation(out=t_.t[:], in_=y_.t[:], func=AF.Copy, scale=sb_.t[:, 2:3]),
                       reads=[y_, sb_], writes=[t_])

                def s5():
                    xw = xnw[cnt['xnw'] % 3]
                    cnt['xnw'] += 1
                    stt['xw'] = xw
                    t_ = stt['t']
                    x_ = stt['x']
                    do(P, 'pool', lambda e: e.tensor_tensor(out=t_.t[:], in0=t_.t[:], in1=GG.t[:, 0, :], op=ALU.mult),
                       reads=[GG], writes=[t_])
                    do(P, 'pool', lambda e: e.tensor_tensor(out=xw.t[:], in0=t_.t[:], in1=x_.t[:], op=ALU.add),
                       reads=[t_, x_], writes=[xw])
                    dma(P, 'pool', y_d[j * 128:(j + 1) * 128, :], xw.t[:], reads=[xw], writes=[ydr[j]], owner=xw)

                def s9():
                    xn_ = xn[cnt['xn'] % 2]
                    cnt['xn'] += 1
                    stt['xn'] = xn_
                    xw = stt['xw']
                    sb2 = stt['st2']
                    do(P, 'dve', lambda e: e.tensor_scalar(out=xn_.t[:], in0=xw.t[:], scalar1=sb2.t[:, 2:3], scalar2=None, op0=ALU.mult),
                       reads=[xw, sb2], writes=[xn_])

                def s10():
                    xn_ = stt['xn']
                    for dt in range(NDT):
                        do(P, 'pe', lambda e, dt=dt: e.transpose(out=tp.t[:, dt, :], in_=xn_.t[:, dt * 128:(dt + 1) * 128],
                                                                 identity=c['ident'].t[:]),
                           reads=[xn_, c['ident']], writes=[tp], inc=(dt == NDT - 1))

                def s11():
                    for dt in range(NDT):
                        do(P, 'dve', lambda e, dt=dt: e.tensor_scalar(out=hT_.t[:, dt, i * 128:(i + 1) * 128], in0=tp.t[:, dt, :],
                                                                      scalar1=abcol.t[:, l, 2, dt:dt + 1],
                                                                      scalar2=abcol.t[:, l, 3, dt:dt + 1],
                                                                      op0=ALU.mult, op1=ALU.add),
                           reads=[tp, abcol], writes=[hT_])
                return [s0] + mk_stats('y', 'st1') + [s4, s5] + mk_stats('xw', 'st2') + [s9, s10, s11]

            def load_oTb(tb):
                dma(P, 'sp', oTb.t[:], oT_d.rearrange("(et p) t -> p et t", p=128)[:, :, tb * 512:(tb + 1) * 512],
                    writes=[oTb], owner=oTb)

            nw = [0]

            def F1(tb):
                hT_ = h2T[tb % 2]
                sched = {}
                if tb + 1 < NB:
                    for i_ in range(4):
                        for k2, fn_ in enumerate(T_stages(tb + 1, i_)):
                            sched.setdefault(8 * i_ + k2, []).append(fn_)
                info = {}
                for it in range(NFT + 2):
                    if it < NFT:
                        ft = it
                        k_ = nw[0]
                        nw[0] += 1
                        wg_ = wgr[k_ % 4]
                        wu_ = wur[k_ % 4]
                        g_ = gps[k_ % 2]
                        gb = gbuf[k_ % 2]
                        cv_ = cv[k_ % 3]
                        sg_ = sg[k_ % 3]
                        info[ft] = (wu_, cv_, sg_)
                        dma(P, 'sp', wg_.t[:], wgb_d[ft], reads=[wg_b], writes=[wg_], owner=wg_)
                        dma(P, 'sp', wu_.t[:], wub_d[ft], reads=[wu_b], writes=[wu_], owner=wu_)
                        for dt in range(NDT):
                            do(P, 'pe', lambda e, dt=dt: e.matmul(g_.t[:], lhsT=wg_.t[:, dt, :], rhs=hT_.t[:, dt, :],
                                                                  start=(dt == 0), stop=(dt == NDT - 1)),
                               reads=[wg_, hT_], writes=[g_], inc=(dt == NDT - 1))
                        do(P, 'act', lambda e: e.activation(out=gb.t[:, 2:514], in_=g_.t[:], func=AF.Copy),
                           reads=[g_], writes=[gb])
                        do(P, 'act', lambda e: e.activation(out=cv_.t[:], in_=g_.t[:], func=AF.Identity,
                                                            scale=wcv.t[:, ft, 2:3], bias=bcv.t[:, ft:ft + 1]),
                           reads=[g_, wcv, bcv], writes=[cv_])
                        do(P, 'act', lambda e: e.activation(out=gb.t[:, 0:2], in_=halo.t[:, ft, :], func=AF.Copy),
                           reads=[halo], writes=[gb])
                        do(P, 'act', lambda e: e.activation(out=halo.t[:, ft, :], in_=g_.t[:, 510:512], func=AF.Copy),
                           reads=[g_], writes=[halo])
                        do(P, 'dve', lambda e: e.scalar_tensor_tensor(
                            out=cv_.t[:], in0=gb.t[:, 1:513], scalar=wcv.t[:, ft, 1:2], in1=cv_.t[:],
                            op0=ALU.mult, op1=ALU.add), reads=[gb, wcv, cv_], writes=[cv_])
                        do(P, 'dve', lambda e: e.scalar_tensor_tensor(
                            out=cv_.t[:], in0=gb.t[:, 0:512], scalar=wcv.t[:, ft, 0:1], in1=cv_.t[:],
                            op0=ALU.mult, op1=ALU.add), reads=[gb, wcv, cv_], writes=[cv_])
                    if 1 <= it <= NFT:
                        _wu, pcv, psg = info[it - 1]
                        do(P, 'act', lambda e: e.activation(out=psg.t[:], in_=pcv.t[:], func=AF.Exp, scale=-1.0),
                           reads=[pcv], writes=[psg])
                        do(P, 'act', lambda e: e.activation(out=psg.t[:], in_=psg.t[:], func=AF.Ln, bias=1.0),
                           reads=[psg], writes=[psg])
                        do(P, 'act', lambda e: e.activation(out=psg.t[:], in_=psg.t[:], func=AF.Exp, scale=-1.0),
                           reads=[psg], writes=[psg])
                        do(P, 'pool', lambda e: e.tensor_tensor(out=psg.t[:], in0=psg.t[:], in1=pcv.t[:], op=ALU.mult),
                           reads=[pcv], writes=[psg])
                    for fn_ in sched.pop(2 * it, []):
                        fn_()
                    if 2 <= it:
                        pft = it - 2
                        pwu, _cv, psg = info.pop(pft)
                        for dt in range(NDT):
                            do(P, 'pe', lambda e, dt=dt: e.matmul(ups.t[:], lhsT=pwu.t[:, dt, :], rhs=hT_.t[:, dt, :],
                                                                  start=(dt == 0), stop=(dt == NDT - 1)),
                               reads=[pwu, hT_], writes=[ups], inc=(dt == NDT - 1))
                        do(P, 'dve', lambda e: e.tensor_tensor(out=aT.t[:, pft, :], in0=ups.t[:], in1=psg.t[:], op=ALU.mult),
                           reads=[psg, ups], writes=[aT])
                    for fn_ in sched.pop(2 * it + 1, []):
                        fn_()
                assert not sched

            def F2(tb, i):
                j = tb * 4 + i
                y_ = yps[cnt['y'] % 2]
                cnt['y'] += 1
                for hf in range(2):
                    for ft in range(NFT):
                        do(P, 'pe', lambda e, ft=ft, hf=hf: e.matmul(
                            y_.t[:, hf * 512:(hf + 1) * 512], lhsT=aT.t[:, ft, i * 128:(i + 1) * 128],
                            rhs=Wd.t[:, ft, hf * 512:(hf + 1) * 512], start=(ft == 0), stop=(ft == NFT - 1)),
                           reads=[aT, Wd], writes=[y_], inc=(ft == NFT - 1 and hf == 1))
                x_ = xt[cnt['xt'] % 3]
                cnt['xt'] += 1
                dma(P, 'pool', x_.t[:], y_d[j * 128:(j + 1) * 128, :], reads=[ydr[j]], writes=[x_], owner=x_)
                rstd, sb_ = rms_stats(y_.t[:], y_, junk, 1.0 / D)
                t_ = t1[cnt['t1'] % 2]
                cnt['t1'] += 1
                do(P, 'dve', lambda e: e.scalar_tensor_tensor(out=t_.t[:], in0=y_.t[:], scalar=rstd, in1=GG.t[:, 1, :],
                                                              op0=ALU.mult, op1=ALU.mult),
                   reads=[y_, sb_, GG], writes=[t_])
                xo_ = xo[cnt['xo'] % 2]
                cnt['xo'] += 1
                do(P, 'pool', lambda e: e.tensor_tensor(out=xo_.t[:], in0=t_.t[:], in1=x_.t[:], op=ALU.add),
                   reads=[t_, x_], writes=[xo_])
                dma(P, 'pool', y_d[j * 128:(j + 1) * 128, :], xo_.t[:], reads=[xo_], writes=[ydr[j]], owner=xo_)

            load_oTb(0)
            wavefront([T_stages(0, i) for i in range(4)], 3)
            for tb in range(NB):
                if tb + 1 < NB:
                    load_oTb(tb + 1)
                F1(tb)
                for i in range(4):
                    F2(tb, i)
            SC.close()
        G.close()
    return nc, P


def make_consts():
    kp = np.arange(128)[:, None]
    qf = np.arange(128)[None, :]
    ident = np.eye(128, dtype=np.float32)
    msk_sb = np.where(kp < qf, 0.0, NEG).astype(np.float32)
    msk_fox = np.where(kp <= qf, 0.0, NEG).astype(np.float32)
    Linc = (kp >= qf).astype(np.float32)
    Lcomp = (1.0 - Linc).astype(np.float32)
    U = (kp <= qf).astype(np.float32)
    ones = np.ones((128, 128), np.float32)
    return np.stack([ident, msk_sb, msk_fox, Linc, Lcomp, U, ones]).astype(np.float32)


def layout_inputs(inp, S, DEPTH):
    f = lambda a: np.ascontiguousarray(np.asarray(a, dtype=np.float32))
    L = DEPTH
    NF = max(DEPTH // 2, 1)
    shared = {}
    wm_ = np.asarray(inp["w_mod"])[:L]
    shared["w_mod"] = f(wm_.reshape(L, NDT, 128, 6 * D).transpose(0, 2, 1, 3))
    ng = np.concatenate([wm_[:, :, 0:2 * D], wm_[:, :, 3 * D:5 * D]], axis=2)
    shared["w_modT"] = f(ng.transpose(0, 2, 1).reshape(L, 32, 128, D).transpose(0, 2, 1, 3))
    bm = np.asarray(inp["b_mod"])[:L]
    shared["b_mod_col"] = f(bm.reshape(L, 48, 128).transpose(0, 2, 1))
    shared["b_mod"] = f(bm)
    gpre = np.stack([np.asarray(inp["g_mix_pre"])[:L], np.asarray(inp["g_ffn_pre"])[:L]], axis=1)
    shared["g_pre"] = f(gpre.reshape(L, 2, NDT, 128).transpose(0, 3, 1, 2))
    shared["g_post"] = f(np.stack([np.asarray(inp["g_mix_post"])[:L], np.asarray(inp["g_ffn_post"])[:L]], axis=1))
    shared["w_qkv"] = f(np.asarray(inp["w_qkv"])[:L].reshape(L, NDT, 128, 3 * D).transpose(0, 2, 1, 3))
    shared["w_o"] = f(np.asarray(inp["w_o"])[:L].reshape(L, NDT, 128, D).transpose(0, 2, 1, 3))
    wfg = np.asarray(inp["w_fg"])
    bfg = np.asarray(inp["b_fg"])
    shared["w_fg"] = f(wfg[:NF].reshape(NF, NDT, 128, NH).transpose(0, 2, 1, 3))
    shared["b_fg"] = f(bfg[:NF])
    for k, nm in (("w_ffn_gate", "w_g"), ("w_ffn_up", "w_u")):
        shared[nm] = f(np.asarray(inp[k])[:L].reshape(L, NDT, 128, NFT, 128).transpose(0, 3, 2, 1, 4))
    shared["w_d"] = f(np.asarray(inp["w_ffn_down"])[:L].reshape(L, NFT, 128, D).transpose(0, 2, 1, 3))
    shared["w_conv"] = f(np.asarray(inp["w_conv"])[:L].reshape(L, 3, NFT, 128).transpose(0, 3, 2, 1))
    shared["b_conv"] = f(np.asarray(inp["b_conv"])[:L].reshape(L, NFT, 128).transpose(0, 2, 1))
    shared["consts"] = make_consts()
    x = np.asarray(inp["x"])
    cc = np.asarray(inp["c"])
    maps = []
    for b in range(x.shape[0]):
        m = dict(shared)
        m["x"] = f(x[b, :S])
        m["c"] = f(cc[b].reshape(NDT, 128).T)
        m["c_row"] = f(cc[b].reshape(1, D))
        maps.append(m)
    return maps


def kernel(**inputs):
    x = np.asarray(inputs["x"])
    B, S, _ = x.shape
    DEPTH = np.asarray(inputs["w_qkv"]).shape[0]
    nc, _ = build_program(S, DEPTH)
    maps = layout_inputs(inputs, S, DEPTH)
    res = run_bass_kernel_spmd(nc, maps, core_ids=list(range(B)))
    out = np.stack([np.asarray(res.results[b]["y"], dtype=np.float32) for b in range(B)], axis=0)
    return out
```

```python
import numpy as np
import ml_dtypes
from contextlib import ExitStack
import concourse.bass as bass
import concourse.mybir as mybir
from concourse.bass_utils import run_bass_kernel_spmd

F32 = mybir.dt.float32
BF16 = mybir.dt.bfloat16
AF = mybir.ActivationFunctionType
ALU = mybir.AluOpType
AX = mybir.AxisListType

D = 1024
NH = 16
DH = 64
FF = 2816
NFT = FF // 128
NDT = D // 128
EPS = 1e-6
QB = 512
NEG = -30000.0


class Sem:
    _n = 0

    def __init__(self, h):
        self.h = h
        Sem._n += 1
        self.key = Sem._n


class Prog:
    ENG = ['pe', 'act', 'dve', 'pool', 'sp']

    def __init__(self, nc, es):
        self.nc = nc
        self.eng = {'pe': nc.tensor, 'act': nc.scalar, 'dve': nc.vector, 'pool': nc.gpsimd, 'sp': nc.sync}
        self.sem = {n: Sem(es.enter_context(nc.semaphore("s_" + n))) for n in self.ENG}
        self.cnt = {n: 0 for n in self.ENG}
        self.seen = {n: {} for n in self.ENG}
        self.streams = []
        self.free_streams = {'sw': [], 'hw': []}
        self.es = es
        self.bar = Sem(es.enter_context(nc.semaphore("s_bar")))
        self.barcnt = 0
        self.nins = 0
        self._uid = 0

    def uid(self):
        self._uid += 1
        return self._uid

    def waits(self, qn, ws):
        seen = self.seen[qn]
        e = self.eng[qn]
        for w in ws:
            if w is None:
                continue
            s, v = w
            if v <= 0 or seen.get(s.key, 0) >= v:
                continue
            seen[s.key] = v
            e.wait_ge(s.h, v)
            self.nins += 1

    def op(self, qn, fn, ws=(), inc=True):
        self.waits(qn, ws)
        ins = fn(self.eng[qn])
        self.nins += 1
        if inc:
            ins.then_inc(self.sem[qn].h, 1)
            self.cnt[qn] += 1
            return (self.sem[qn], self.cnt[qn])
        return None

    def dma(self, qn, out, in_, st, ws=()):
        self.waits(qn, ws)
        st.cnt += 16
        self.eng[qn].dma_start(out=out, in_=in_).then_inc(st.sem.h, 16)
        self.nins += 1
        return (st.sem, st.cnt)

    def barrier(self):
        ws = [(self.sem[n], self.cnt[n]) for n in self.ENG]
        ws += [(st.sem, st.cnt) for st in self.streams if st.cnt > 0]
        self.waits('sp', ws)
        self.barcnt += 16
        self.eng['sp'].dma_start(out=self.bar_dst, in_=self.bar_src).then_inc(self.bar.h, 16)
        for n in self.ENG:
            self.waits(n, [(self.bar, self.barcnt)])

    def get_stream(self, kind):
        if self.free_streams[kind]:
            return self.free_streams[kind].pop()
        st = DmaStream(self, "d%s%d" % (kind, self.uid()))
        st.kind = kind
        return st


class DmaStream:
    def __init__(self, P, name):
        self.sem = Sem(P.es.enter_context(P.nc.semaphore(name)))
        self.cnt = 0
        P.streams.append(self)


class Buf:
    def __init__(self, P, es, t=None, name=None):
        self.P = P
        self.es = es
        self.t = t
        self.w = None
        self.r = {}
        self.name = name or ("b%d" % P.uid())
        self._ds = {}

    def ds(self, kind):
        if kind not in self._ds:
            self._ds[kind] = self.P.get_stream(kind)
        return self._ds[kind]


class Scope:
    def __init__(self, P, tag):
        self.P = P
        self.es = ExitStack()
        self.tag = tag
        self.bufs = []

    def sb(self, name, shape, dt):
        t = self.es.enter_context(self.P.nc.sbuf_tensor("%s_%s_%d" % (self.tag, name, self.P.uid()), shape, dt))
        b = Buf(self.P, self.es, t, name)
        self.bufs.append(b)
        return b

    def ps(self, name, shape, dt):
        t = self.es.enter_context(self.P.nc.psum_tensor("%s_%s_%d" % (self.tag, name, self.P.uid()), shape, dt))
        b = Buf(self.P, self.es, t, name)
        self.bufs.append(b)
        return b

    def close(self):
        self.P.barrier()
        for b in self.bufs:
            for kind, st in b._ds.items():
                self.P.free_streams[kind].append(st)
            b._ds = {}
        self.es.close()


def _rw(reads, writes):
    ws = []
    for b in reads:
        ws.append(b.w)
    for b in writes:
        ws.append(b.w)
        ws.extend(b.r.values())
    return ws


def _upd(tok, reads, writes):
    for b in reads:
        if b in writes:
            continue
        o = b.r.get(tok[0].key)
        if o is None or o[1] < tok[1]:
            b.r[tok[0].key] = tok
    for b in writes:
        b.w = tok
        b.r = {}


def wavefront(stage_lists, stride):
    sched = {}
    for i, st in enumerate(stage_lists):
        for k, fn in enumerate(st):
            sched.setdefault(stride * i + k, []).append(fn)
    for slot in sorted(sched):
        for fn in sched[slot]:
            fn()


def do(P, qn, fn, reads=(), writes=(), inc=True):
    ws = _rw(reads, writes)
    if qn == 'pe':
        pes = P.sem['pe']
        ws = [w for w in ws if w is not None and w[0] is not pes]
    tok = P.op(qn, fn, ws, inc=inc)
    if tok is not None:
        _upd(tok, reads, writes)
    return tok


def dma(P, qn, out_ap, in_ap, reads=(), writes=(), owner=None):
    ws = _rw(reads, writes)
    tok = P.dma(qn, out_ap, in_ap, owner.ds('sw' if qn == 'pool' else 'hw'), ws)
    _upd(tok, reads, writes)
    return tok


class AttnStream:
    def __init__(self, sc, S, fox, tag, nz, nC, no):
        self.S = S
        self.z = [sc.ps("z%s%d" % (tag, i), [128, 512], F32) for i in range(nz)]
        self.C = [sc.ps("C%s%d" % (tag, i), [128, 512], F32) for i in range(nC)] if not fox else []
        self.o = [sc.ps("o%s%d" % (tag, i), [128, 512], F32) for i in range(no)]
        if not fox:
            self.e = [sc.sb("e%s%d" % (tag, i), [128, 512], F32) for i in range(3)]
            self.sp = [sc.sb("sp%s%d" % (tag, i), [128, 512], BF16) for i in range(3)]
            self.E2 = [sc.sb("E2%s%d" % (tag, i), [128, 512], F32) for i in range(2)]
        else:
            self.rden = [sc.sb("rden%s%d" % (tag, i), [128, 512], F32) for i in range(2)]
            self.VO = [sc.sb("VO%s%d" % (tag, i), [128, S // 128, 128], BF16) for i in range(2)]
        self.A = [sc.sb("A%s%d" % (tag, i), [128, 512], BF16) for i in range(3)]
        self.qT = [sc.sb("qT%s%d" % (tag, i), [66, S], BF16) for i in range(2)]
        self.kT = [sc.sb("kT%s%d" % (tag, i), [66, S], BF16) for i in range(2)]
        self.oTq = [sc.sb("oTq%s%d" % (tag, i), [64, 512], BF16) for i in range(2)]


def emit_attention(P, streams, c, fox, load_head, store_q, v_of, bias_of=None, background=()):
    S = streams[0][0].S
    nqb = S // QB
    tpq = QB // 128
    msk = c['msk_fox'] if fox else c['msk_sb']
    KD = 66 if fox else 64

    class Ctx:
        pass

    ctxs = []
    for si, (AB, heads) in enumerate(streams):
        cx = Ctx()
        cx.AB = AB
        cx.si = si
        cx.heads = heads
        cx.steps = []
        for hi, h in enumerate(heads):
            for qb in range(nqb):
                jd = (qb + 1) * tpq - 1
                for j in range(jd, -1, -1):
                    m = j - qb * tpq
                    cx.steps.append(dict(hi=hi, h=h, qb=qb, j=j, c0=(128 * m if m >= 0 else 0), diag=(m >= 0),
                                         first=(j == jd), last=(j == 0), qbg=hi * nqb + qb))
        cx.n = len(cx.steps)
        cx.loaded = set()
        ctxs.append(cx)

    def maybe_load(cx, hi):
        if hi < len(cx.heads) and hi not in cx.loaded:
            cx.loaded.add(hi)
            load_head(cx.AB, cx.heads[hi], hi % 2)

    def QK(cx, s):
        AB = cx.AB
        st = cx.steps[s]
        sl = st['hi'] % 2
        z = AB.z[s % len(AB.z)]
        c0 = st['c0']
        q0 = st['qb'] * QB
        j = st['j']
        qT = AB.qT[sl]
        kT = AB.kT[sl]
        do(P, 'pe', lambda e: e.matmul(z.t[:, c0:QB], lhsT=kT.t[0:KD, j * 128:(j + 1) * 128],
                                       rhs=qT.t[0:KD, q0 + c0:q0 + QB],
                                       start=True, stop=True, skip_group_check=True),
           reads=[qT, kT] + ([c['ident'], msk] if st['diag'] else []), writes=[z], inc=not st['diag'])
        if st['diag']:
            do(P, 'pe', lambda e: e.matmul(z.t[:, c0:c0 + 128], lhsT=c['ident'].t[:], rhs=msk.t[:],
                                           start=False, stop=True, skip_group_check=True),
               reads=[qT, kT, c['ident'], msk], writes=[z])

    def PV(cx, s):
        AB = cx.AB
        st = cx.steps[s]
        c0 = st['c0']
        j = st['j']
        o = AB.o[st['qbg'] % len(AB.o)]
        A = AB.A[s % 3]
        vap, vbuf = v_of(AB, st['h'], st['hi'] % 2, j)
        if not fox:
            do(P, 'pe', lambda e: e.matmul(o.t[0:64, c0:QB], lhsT=vap, rhs=A.t[:, c0:QB],
                                           start=st['first'], stop=st['last'], skip_group_check=True),
               reads=[vbuf, A], writes=[o])
        else:
            do(P, 'pe', lambda e: e.matmul(o.t[:, c0:QB], lhsT=vap, rhs=A.t[:, c0:QB],
                                           start=st['first'], stop=st['last'], skip_group_check=True),
               reads=[vbuf, A], writes=[o])

    def EVAC(cx, s):
        AB = cx.AB
        st = cx.steps[s]
        o = AB.o[st['qbg'] % len(AB.o)]
        oq = AB.oTq[st['qbg'] % 2]
        if not fox:
            do(P, 'dve', lambda e: e.tensor_copy(out=oq.t[:], in_=o.t[0:64, :]), reads=[o], writes=[oq])
        else:
            rd = AB.rden[st['qbg'] % 2]
            do(P, 'dve', lambda e: e.reciprocal(out=rd.t[64:128, :], in_=o.t[64:128, :]), reads=[o], writes=[rd])
            do(P, 'dve', lambda e: e.tensor_tensor(out=oq.t[:], in0=o.t[0:64, :], in1=rd.t[64:128, :], op=ALU.mult),
               reads=[o, rd], writes=[oq])
        store_q(oq, st['h'], st['qb'])

    def tick(cx, t):
        AB = cx.AB
        steps = cx.steps
        n = cx.n
        if not fox:
            if 0 <= t - 1 < n and not steps[t - 1]['last']:
                s = t - 1
                st = steps[s]
                C = AB.C[st['qbg'] % len(AB.C)]
                sp = AB.sp[s % 3]
                c0 = st['c0']
                do(P, 'pe', lambda e: e.matmul(C.t[:, c0:QB], lhsT=c['Lcomp'].t[:], rhs=sp.t[:, c0:QB],
                                               start=False, stop=True, skip_group_check=True),
                   reads=[sp, c['Lcomp']], writes=[C])
            if 0 <= t < n:
                s = t
                st = steps[s]
                C = AB.C[st['qbg'] % len(AB.C)]
                sp = AB.sp[s % 3]
                c0 = st['c0']
                do(P, 'pe', lambda e: e.matmul(C.t[:, c0:QB], lhsT=c['Linc'].t[:], rhs=sp.t[:, c0:QB],
                                               start=st['first'], stop=True, skip_group_check=True),
                   reads=[sp, c['Linc']], writes=[C])
        if 0 <= t - 1 < n:
            PV(cx, t - 1)
        if 0 <= t + 2 < n:
            QK(cx, t + 2)
        if not fox:
            if 0 <= t + 1 < n:
                s = t + 1
                st = steps[s]
                z = AB.z[s % len(AB.z)]
                ee = AB.e[s % 3]
                c0 = st['c0']
                do(P, 'act', lambda e: e.activation(out=ee.t[:, c0:QB], in_=z.t[:, c0:QB], func=AF.Exp, scale=0.125),
                   reads=[z], writes=[ee])
            if 0 <= t < n:
                s = t
                st = steps[s]
                C = AB.C[st['qbg'] % len(AB.C)]
                E2 = AB.E2[s % 2]
                c0 = st['c0']
                do(P, 'act', lambda e: e.activation(out=E2.t[:, c0:QB], in_=C.t[:, c0:QB], func=AF.Exp, scale=-1.0),
                   reads=[C], writes=[E2])
            if 0 <= t + 1 < n:
                s = t + 1
                st = steps[s]
                ee = AB.e[s % 3]
                sp = AB.sp[s % 3]
                c0 = st['c0']
                do(P, 'act', lambda e: e.activation(out=sp.t[:, c0:QB], in_=ee.t[:, c0:QB], func=AF.Ln, bias=1.0, scale=1.0),
                   reads=[ee], writes=[sp])
        else:
            if 0 <= t + 1 < n:
                s = t + 1
                st = steps[s]
                z = AB.z[s % len(AB.z)]
                A = AB.A[s % 3]
                c0 = st['c0']
                bap, bbuf = bias_of(st['h'], st['qb'], st['j'])
                do(P, 'act', lambda e: e.activation(out=A.t[:, c0:QB], in_=z.t[:, c0:QB], func=AF.Exp, scale=0.125, bias=bap),
                   reads=[z, bbuf], writes=[A])
        if 0 <= t - 1 < n and steps[t - 1]['last']:
            EVAC(cx, t - 1)
        if not fox and 0 <= t < n:
            s = t
            st = steps[s]
            ee = AB.e[s % 3]
            E2 = AB.E2[s % 2]
            A = AB.A[s % 3]
            c0 = st['c0']
            do(P, 'dve', lambda e: e.tensor_tensor(out=A.t[:, c0:QB], in0=ee.t[:, c0:QB], in1=E2.t[:, c0:QB], op=ALU.mult),
               reads=[ee, E2], writes=[A])
        if 0 <= t < n and (t == 0 or steps[t]['hi'] != steps[t - 1]['hi']):
            maybe_load(cx, steps[t]['hi'] + 1)

    for cx in ctxs:
        maybe_load(cx, 0)
    nmax = max(cx.n for cx in ctxs)
    background = list(background)
    for t in range(-2, nmax + 2):
        for cx in ctxs:
            tick(cx, t)
        if background and t >= 4 and t % 4 == 0:
            background.pop(0)()
    for fn in background:
        fn()


CONST_NAMES = ['ident', 'msk_sb', 'msk_fox', 'Linc', 'Lcomp', 'U', 'ones']


def build_program(S, DEPTH, stop_after=None):
    NT = S // 128
    NB = S // QB
    NF = DEPTH // 2
    nc = bass.Bass("TRN2", target_bir_lowering=False)

    def din(name, shape, dt=F32):
        return nc.dram_tensor(name, shape, dt, kind="ExternalInput").ap()

    x_d = din("x", [S, D])
    c_d = din("c", [128, NDT])
    wmod_d = din("w_mod", [DEPTH, 128, NDT, 6 * D])
    wmodT_d = din("w_modT", [DEPTH, 128, 32, D])
    crow_d = din("c_row", [1, D])
    bmodc_d = din("b_mod_col", [DEPTH, 128, 48])
    bmod_d = din("b_mod", [DEPTH, 6 * D])
    gpre_d = din("g_pre", [DEPTH, 128, 2, NDT])
    gpost_d = din("g_post", [DEPTH, 2, D])
    wqkv_d = din("w_qkv", [DEPTH, 128, NDT, 3 * D])
    wo_d = din("w_o", [DEPTH, 128, NDT, D])
    wfg_d = din("w_fg", [max(NF, 1), 128, NDT, NH])
    bfg_d = din("b_fg", [max(NF, 1), NH])
    wg_d = din("w_g", [DEPTH, NFT, 128, NDT, 128])
    wu_d = din("w_u", [DEPTH, NFT, 128, NDT, 128])
    wd_d = din("w_d", [DEPTH, 128, NFT, D])
    wcv_d = din("w_conv", [DEPTH, 128, NFT, 3])
    bcv_d = din("b_conv", [DEPTH, 128, NFT])
    cst_d = din("consts", [len(CONST_NAMES), 128, 128])
    y_d = nc.dram_tensor("y", [S, D], F32, kind="ExternalOutput").ap()
    qT_d = nc.dram_tensor("qT_s", [D, S], BF16, kind="Internal").ap()
    kT_d = nc.dram_tensor("kT_s", [D, S], BF16, kind="Internal").ap()
    oT_d = nc.dram_tensor("oT_s", [D, S], BF16, kind="Internal").ap()
    aug_d = nc.dram_tensor("aug_s", [NH, 2, S], BF16, kind="Internal").ap()
    gg_d = nc.dram_tensor("gg_s", [DEPTH, 2, 128, D], F32, kind="Internal").ap()
    bar_d = nc.dram_tensor("bar_s", [2, 16], F32, kind="Internal").ap()
    wqkvb_d = nc.dram_tensor("wqkvb_s", [128, NDT, 3 * D], BF16, kind="Internal").ap()
    wob_d = nc.dram_tensor("wob_s", [128, NDT, D], BF16, kind="Internal").ap()
    wgb_d = nc.dram_tensor("wgb_s", [NFT, 128, NDT, 128], BF16, kind="Internal").ap()
    wub_d = nc.dram_tensor("wub_s", [NFT, 128, NDT, 128], BF16, kind="Internal").ap()
    wdb_d = nc.dram_tensor("wdb_s", [128, NFT, D], BF16, kind="Internal").ap()

    es = ExitStack()
    with es:
        P = Prog(nc, es)
        P.bar_dst = bar_d[0:1, :]
        P.bar_src = cst_d[0, 0:1, 0:16]
        G = Scope(P, "g")
        c = {}
        for i, nm in enumerate(['ident', 'msk_sb', 'msk_fox', 'Linc', 'Lcomp']):
            c[nm] = G.sb("c_" + nm, [128, 128], BF16)
            dma(P, 'pool', c[nm].t[:], cst_d[i], writes=[c[nm]], owner=c[nm])
        for i, nm in [(0, 'ident_f'), (5, 'U_f'), (6, 'ones_f')]:
            c[nm] = G.sb("c_" + nm, [128, 128], F32)
            dma(P, 'sp', c[nm].t[:], cst_d[i], writes=[c[nm]], owner=c[nm])
        c['ones64'] = G.sb("ones64", [128, 64], BF16)
        do(P, 'dve', lambda e: e.memset(c['ones64'].t[:], 1.0), writes=[c['ones64']])
        neghalf = G.sb("neghalf", [128, 1], F32)
        do(P, 'dve', lambda e: e.memset(neghalf.t[:], -0.5), writes=[neghalf])
        wq_b = Buf(P, None, None, "wq_b")
        wo_b = Buf(P, None, None, "wo_b")
        wg_b = Buf(P, None, None, "wg_b")
        wu_b = Buf(P, None, None, "wu_b")
        wd_b = Buf(P, None, None, "wd_b")
        G.bufs += [wq_b, wo_b, wg_b, wu_b, wd_b]

        def cast_wqkv_list(l_):
            return [lambda dt=dt: dma(P, 'pool', wqkvb_d[:, dt, :], wqkv_d[l_, :, dt, :], writes=[wq_b], owner=wq_b)
                    for dt in range(NDT)]

        def cast_wqkv(l_):
            for fn in cast_wqkv_list(l_):
                fn()

        def cast_ffn_list(l_):
            fl_ = [lambda: dma(P, 'pool', wob_d, wo_d[l_], writes=[wo_b], owner=wo_b)]
            for q4 in range(0, NFT, 2):
                fl_.append(lambda q4=q4: dma(P, 'pool', wdb_d[:, q4:q4 + 2, :], wd_d[l_, :, q4:q4 + 2, :], writes=[wd_b], owner=wd_b))
            for q4 in range(0, NFT, 2):
                fl_.append(lambda q4=q4: dma(P, 'pool', wgb_d[q4:q4 + 2], wg_d[l_, q4:q4 + 2], writes=[wg_b], owner=wg_b))
                fl_.append(lambda q4=q4: dma(P, 'pool', wub_d[q4:q4 + 2], wu_d[l_, q4:q4 + 2], writes=[wu_b], owner=wu_b))
            return fl_

        cast_wqkv(0)
        abcol = G.sb("abcol", [128, DEPTH, 4, NDT], F32)
        stat = [G.sb("stat%d" % i, [128, 4], F32) for i in range(8)]
        statn = [0]

        def rms_stats(src_ap, srcbuf, junk, width_scale):
            sb_ = stat[statn[0] % len(stat)]
            statn[0] += 1
            do(P, 'act', lambda e: e.activation(out=junk.t[:], in_=src_ap, func=AF.Square, accum_out=sb_.t[:, 0:1]),
               reads=[srcbuf], writes=[junk, sb_])
            do(P, 'act', lambda e: e.activation(out=sb_.t[:, 1:2], in_=sb_.t[:, 0:1], func=AF.Ln, scale=width_scale, bias=EPS),
               reads=[sb_], writes=[sb_])
            do(P, 'act', lambda e: e.activation(out=sb_.t[:, 2:3], in_=sb_.t[:, 1:2], func=AF.Exp, scale=-0.5),
               reads=[sb_], writes=[sb_])
            return sb_.t[:, 2:3], sb_

        PR = Scope(P, "pr")
        cT = PR.sb("cT", [128, NDT], F32)
        cact = PR.sb("cact", [128, NDT], F32)
        crep = PR.sb("crep", [128, NDT, 128], F32)
        dma(P, 'sp', cT.t[:], c_d, writes=[cT], owner=cT)
        do(P, 'act', lambda e: e.activation(out=cact.t[:], in_=cT.t[:], func=AF.Silu), reads=[cT], writes=[cact])
        for dt in range(NDT):
            do(P, 'dve', lambda e, dt=dt: e.tensor_scalar(out=crep.t[:, dt, :], in0=c['ones_f'].t[:], scalar1=cact.t[:, dt:dt + 1],
                                                          scalar2=None, op0=ALU.mult), reads=[cact, c['ones_f']], writes=[crep])
        wm = [PR.sb("wm%d" % i, [128, NDT, 512], F32) for i in range(2)]
        wmT = [PR.sb("wmT%d" % i, [128, 4, D], F32) for i in range(3)]
        cbc = PR.sb("cbc", [128, D], F32)
        junkf = PR.sb("junkf", [128, D], F32)
        dma(P, 'sp', cbc.t[:], crow_d[0, :].partition_broadcast(128), writes=[cbc], owner=cbc)
        do(P, 'act', lambda e: e.activation(out=cbc.t[:], in_=cbc.t[:], func=AF.Silu), reads=[cbc], writes=[cbc])
        modcol = PR.sb("modcol", [128, 48], F32)
        nchT = 0
        ggps = [PR.ps("ggps%d" % i, [128, 512], F32) for i in range(2)]
        bmodc = PR.sb("bmodc", [128, 48], F32)
        modc = PR.sb("modc", [128, 48], F32)
        gpre = PR.sb("gpre", [128, 2, NDT], F32)
        bmbc = PR.sb("bmbc", [128, 2, D], F32)
        gpbc = PR.sb("gpbc", [128, 2, D], F32)
        ggst = [PR.sb("ggst%d" % i, [128, 512], F32) for i in range(2)]
        nchunk = 0
        ngg = 0
        for l in range(DEPTH):
            dma(P, 'sp', bmodc.t[:], bmodc_d[l], writes=[bmodc], owner=bmodc)
            dma(P, 'sp', gpre.t[:], gpre_d[l], writes=[gpre], owner=gpre)
            for which, v in enumerate((2, 5)):
                dma(P, 'sp', bmbc.t[:, which, :], bmod_d[l, v * D:(v + 1) * D].partition_broadcast(128),
                    writes=[bmbc], owner=bmbc)
                dma(P, 'sp', gpbc.t[:, which, :], gpost_d[l, which, :].partition_broadcast(128),
                    writes=[gpbc], owner=gpbc)
            for mcb in range(12):
                v = mcb // 2
                half = mcb % 2
                if v in (2, 5):
                    w_ = wm[nchunk % 2]
                    nchunk += 1
                    dma(P, 'pool', w_.t[:], wmod_d[l, :, :, mcb * 512:(mcb + 1) * 512], writes=[w_], owner=w_)
                    which = 0 if v == 2 else 1
                    gp = ggps[ngg % 2]
                    gs = ggst[ngg % 2]
                    ngg += 1
                    for dt in range(NDT):
                        do(P, 'pe', lambda e, dt=dt, gp=gp, w_=w_: e.matmul(gp.t[:], lhsT=crep.t[:, dt, :], rhs=w_.t[:, dt, :],
                                                                            start=(dt == 0), stop=(dt == NDT - 1)),
                           reads=[crep, w_], writes=[gp], inc=(dt == NDT - 1))
                    do(P, 'dve', lambda e, gp=gp, gs=gs, which=which, half=half: e.tensor_tensor(
                        out=gs.t[:], in0=gp.t[:], in1=bmbc.t[:, which, half * 512:(half + 1) * 512], op=ALU.add),
                       reads=[gp, bmbc], writes=[gs])
                    do(P, 'dve', lambda e, gs=gs, which=which, half=half: e.tensor_tensor(
                        out=gs.t[:], in0=gs.t[:], in1=gpbc.t[:, which, half * 512:(half + 1) * 512], op=ALU.mult),
                       reads=[gs, gpbc], writes=[gs])
                    dma(P, 'sp', gg_d[l, which, :, half * 512:(half + 1) * 512], gs.t[:], reads=[gs], owner=gs)
                else:
                    kq = mcb * 4 if mcb < 4 else mcb * 4 - 8
                    wt = wmT[nchT % 3]
                    nchT += 1
                    dma(P, 'sp', wt.t[:], wmodT_d[l, :, kq:kq + 4, :], writes=[wt], owner=wt)
                    for sc_ in range(4):
                        mc = mcb * 4 + sc_
                        do(P, 'dve', lambda e, mc=mc, sc_=sc_, wt=wt: e.scalar_tensor_tensor(
                            out=junkf.t[:], in0=wt.t[:, sc_, :], scalar=1.0, in1=cbc.t[:], op0=ALU.mult, op1=ALU.mult,
                            accum_out=modcol.t[:, mc:mc + 1]), reads=[wt, cbc], writes=[junkf, modcol])
            for (a0, a1) in ((0, 16), (24, 40)):
                do(P, 'dve', lambda e, a0=a0, a1=a1: e.tensor_tensor(out=modc.t[:, a0:a1], in0=modcol.t[:, a0:a1],
                                                                     in1=bmodc.t[:, a0:a1], op=ALU.add),
                   reads=[modcol, bmodc], writes=[modc])
            for which, (sh, scl) in enumerate(((0, 1), (3, 4))):
                do(P, 'dve', lambda e, which=which, scl=scl, l=l: e.scalar_tensor_tensor(
                    out=abcol.t[:, l, 2 * which, :], in0=modc.t[:, scl * 8:(scl + 1) * 8], scalar=1.0, in1=gpre.t[:, which, :],
                    op0=ALU.add, op1=ALU.mult), reads=[modc, gpre], writes=[abcol])
                do(P, 'dve', lambda e, which=which, sh=sh, l=l: e.tensor_copy(
                    out=abcol.t[:, l, 2 * which + 1, :], in_=modc.t[:, sh * 8:(sh + 1) * 8]), reads=[modc], writes=[abcol])
        PR.close()
        if stop_after == ('P',):
            G.close()
            return nc, P

        def norm_transpose(sc, src_ap, srcbuf, xn, junk, tp, hT, col0, l, which):
            rstd, sb_ = rms_stats(src_ap, srcbuf, junk, 1.0 / D)
            do(P, 'dve', lambda e: e.tensor_scalar(out=xn.t[:], in0=src_ap, scalar1=rstd, scalar2=None, op0=ALU.mult),
               reads=[srcbuf, sb_], writes=[xn])
            for dt in range(NDT):
                do(P, 'pe', lambda e, dt=dt: e.transpose(out=tp.t[:, dt, :], in_=xn.t[:, dt * 128:(dt + 1) * 128],
                                                         identity=c['ident'].t[:]),
                   reads=[xn, c['ident']], writes=[tp], inc=(dt == NDT - 1))
            for dt in range(NDT):
                do(P, 'dve', lambda e, dt=dt: e.tensor_scalar(out=hT.t[:, dt, col0:col0 + 128], in0=tp.t[:, dt, :],
                                                              scalar1=abcol.t[:, l, 2 * which, dt:dt + 1],
                                                              scalar2=abcol.t[:, l, 2 * which + 1, dt:dt + 1],
                                                              op0=ALU.mult, op1=ALU.add),
                   reads=[tp, abcol], writes=[hT])

        for l in range(DEPTH):
            fox = (l % 2 == 1)
            fl = l // 2
            xsrc = x_d if l == 0 else y_d
            SAB = Scope(P, "ab%d" % l)
            Vres = SAB.sb("Vres", [128, NT, D], BF16)
            if fox:
                nlf = SAB.sb("nlf", [128, NT, NH], F32)
                tab = SAB.sb("tab", [128, NB, NT, NH], F32)
            SA = Scope(P, "a%d" % l)
            Wqkv = SA.sb("Wqkv", [128, NDT, 3 * D], BF16)
            for dt in range(0, NDT, 2):
                dma(P, 'sp', Wqkv.t[:, dt:dt + 2, :], wqkvb_d[:, dt:dt + 2, :], reads=[wq_b], writes=[Wqkv], owner=Wqkv)
            if fox:
                wfg = SA.sb("wfg", [128, NDT, NH], BF16)
                dma(P, 'pool', wfg.t[:], wfg_d[fl], writes=[wfg], owner=wfg)
                bfg = SA.sb("bfg", [128, NH], F32)
                dma(P, 'sp', bfg.t[:], bfg_d[fl].partition_broadcast(128), writes=[bfg], owner=bfg)
                fsb = [SA.sb("fsb%d" % i, [128, NH], F32) for i in range(2)]
            hT = [SA.sb("hT%d" % i, [128, NDT, 512], BF16) for i in range(2)]
            xt = [SA.sb("xt%d" % i, [128, D], F32) for i in range(3)]
            xn = [SA.sb("xn%d" % i, [128, D], BF16) for i in range(2)]
            junk = SA.sb("junk", [128, D], BF16)
            qks = [SA.sb("qks%d" % i, [128, 4, 512], BF16) for i in range(2)]
            SAP = Scope(P, "ap%d" % l)
            tp = [SAP.ps("tp%d" % i, [128, NDT, 128], BF16) for i in range(2)]
            mm = [SAP.ps("mm%d" % i, [128, 512], F32) for i in range(4)]
            nmm = 0
            nqk = 0
            if fox:
                flps = SAP.ps("flps", [128, NH], F32)
            def A_stages(tb, i):
                j = tb * 4 + i
                h_ = hT[tb % 2]
                x_ = xt[j % 3]
                xn_ = xn[j % 2]
                tp_ = tp[j % 2]
                stt = {}

                def s0():
                    dma(P, 'sp', x_.t[:], xsrc[j * 128:(j + 1) * 128, :], writes=[x_], owner=x_)
                    sb_ = stat[statn[0] % len(stat)]
                    statn[0] += 1
                    stt['sb'] = sb_
                    do(P, 'act', lambda e: e.activation(out=junk.t[:], in_=x_.t[:], func=AF.Square, accum_out=sb_.t[:, 0:1]),
                       reads=[x_], writes=[junk, sb_])

                def s1():
                    sb_ = stt['sb']
                    do(P, 'act', lambda e: e.activation(out=sb_.t[:, 1:2], in_=sb_.t[:, 0:1], func=AF.Ln, scale=1.0 / D, bias=EPS),
                       reads=[sb_], writes=[sb_])

                def s2():
                    sb_ = stt['sb']
                    do(P, 'act', lambda e: e.activation(out=sb_.t[:, 2:3], in_=sb_.t[:, 1:2], func=AF.Exp, scale=-0.5),
                       reads=[sb_], writes=[sb_])

                def s3():
                    sb_ = stt['sb']
                    do(P, 'dve', lambda e: e.tensor_scalar(out=xn_.t[:], in0=x_.t[:], scalar1=sb_.t[:, 2:3], scalar2=None, op0=ALU.mult),
                       reads=[x_, sb_], writes=[xn_])

                def s4():
                    for dt in range(NDT):
                        do(P, 'pe', lambda e, dt=dt: e.transpose(out=tp_.t[:, dt, :], in_=xn_.t[:, dt * 128:(dt + 1) * 128],
                                                                 identity=c['ident'].t[:]),
                           reads=[xn_, c['ident']], writes=[tp_], inc=(dt == NDT - 1))

                def s5():
                    for dt in range(NDT):
                        do(P, 'dve', lambda e, dt=dt: e.tensor_scalar(out=h_.t[:, dt, i * 128:(i + 1) * 128], in0=tp_.t[:, dt, :],
                                                                      scalar1=abcol.t[:, l, 0, dt:dt + 1],
                                                                      scalar2=abcol.t[:, l, 1, dt:dt + 1],
                                                                      op0=ALU.mult, op1=ALU.add),
                           reads=[tp_, abcol], writes=[h_])
                return [s0, s1, s2, s3, s4, s5]

            cntA = dict(mm=0, qk=0)

            def A_groups(tb):
                h_ = hT[tb % 2]
                groups = []
                for g in range(4):
                    for u in range(4):
                        def grp(g=g, u=u):
                            if u == 0:
                                cntA['qs'] = qks[cntA['qk'] % 2]
                                cntA['qk'] += 1
                            qs = cntA['qs']
                            et = g * 4 + u
                            m_ = mm[cntA['mm'] % 4]
                            cntA['mm'] += 1
                            for dt in range(NDT):
                                do(P, 'pe', lambda e, dt=dt: e.matmul(
                                    m_.t[:], lhsT=Wqkv.t[:, dt, et * 128:(et + 1) * 128], rhs=h_.t[:, dt, :],
                                    start=(dt == 0), stop=(dt == NDT - 1)),
                                   reads=[Wqkv, h_], writes=[m_], inc=(dt == NDT - 1))
                            do(P, 'act', lambda e: e.activation(out=qs.t[:, u, :], in_=m_.t[:], func=AF.Copy),
                               reads=[m_], writes=[qs])
                            if u == 3:
                                dst = (qT_d if g < 2 else kT_d)[(g % 2) * 512:(g % 2) * 512 + 512, tb * 512:(tb + 1) * 512]
                                dma(P, 'sp', dst.rearrange("(u p) t -> p u t", p=128), qs.t[:], reads=[qs], owner=qs)
                        groups.append(grp)
                for i in range(4):
                    j = tb * 4 + i
                    for hf in range(2):
                        def grp(i=i, j=j, hf=hf):
                            m_ = mm[cntA['mm'] % 4]
                            cntA['mm'] += 1
                            for dt in range(NDT):
                                do(P, 'pe', lambda e, dt=dt: e.matmul(
                                    m_.t[:], lhsT=h_.t[:, dt, i * 128:(i + 1) * 128],
                                    rhs=Wqkv.t[:, dt, 2 * D + hf * 512:2 * D + (hf + 1) * 512],
                                    start=(dt == 0), stop=(dt == NDT - 1)),
                                   reads=[Wqkv, h_], writes=[m_], inc=(dt == NDT - 1))
                            do(P, 'dve', lambda e: e.tensor_copy(out=Vres.t[:, j, hf * 512:(hf + 1) * 512], in_=m_.t[:]),
                               reads=[m_], writes=[Vres])
                            if fox and hf == 1:
                                for dt in range(NDT):
                                    do(P, 'pe', lambda e, dt=dt: e.matmul(flps.t[:], lhsT=h_.t[:, dt, i * 128:(i + 1) * 128],
                                                                          rhs=wfg.t[:, dt, :], start=(dt == 0), stop=(dt == NDT - 1)),
                                       reads=[wfg, h_], writes=[flps], inc=(dt == NDT - 1))
                                f_ = fsb[j % 2]
                                do(P, 'dve', lambda e: e.tensor_tensor(out=f_.t[:], in0=flps.t[:], in1=bfg.t[:], op=ALU.add),
                                   reads=[flps, bfg], writes=[f_])
                                do(P, 'act', lambda e: e.activation(out=f_.t[:], in_=f_.t[:], func=AF.Exp, scale=-1.0),
                                   reads=[f_], writes=[f_])
                                do(P, 'act', lambda e: e.activation(out=nlf.t[:, j, :], in_=f_.t[:], func=AF.Ln, bias=1.0),
                                   reads=[f_], writes=[nlf])
                        groups.append(grp)
                return groups

            wavefront([A_stages(0, i) for i in range(4)], 2)
            for tb in range(NB):
                sched = {}
                if tb + 1 < NB:
                    for i_ in range(4):
                        for k2, fn_ in enumerate(A_stages(tb + 1, i_)):
                            sched.setdefault(4 * i_ + k2, []).append(fn_)
                for gi, grp in enumerate(A_groups(tb)):
                    grp()
                    for fn_ in sched.pop(gi, []):
                        fn_()
                assert not sched
            SAP.close()
            if fox:
                cps = SA.ps("cps", [128, NT * NH], F32)
                tps = SA.ps("tps", [128, NT * NH], F32)
                cumT = SA.sb("cumT", [128, NT, NH], F32)
                crefs = SA.sb("crefs", [128, NB, NH], F32)
                totT = SA.sb("totT", [128, NH, NT], F32)
                incl = SA.sb("incl", [128, NH, NT], F32)
                smask = SA.sb("smask", [128, NH, NT], F32)
                for j in range(NT):
                    do(P, 'pe', lambda e, j=j: e.matmul(cps.t[:, j * NH:(j + 1) * NH], lhsT=c['U_f'].t[:], rhs=nlf.t[:, j, :],
                                                        start=True, stop=True, skip_group_check=True),
                       reads=[nlf, c['U_f']], writes=[cps], inc=(j == NT - 1))
                for j in range(NT):
                    do(P, 'pe', lambda e, j=j: e.matmul(tps.t[:, j * NH:(j + 1) * NH], lhsT=c['ones_f'].t[:], rhs=nlf.t[:, j, :],
                                                        start=True, stop=True, skip_group_check=True),
                       reads=[nlf, c['ones_f']], writes=[tps], inc=(j == NT - 1))
                do(P, 'dve', lambda e: e.memset(smask.t[:], 1.0), writes=[smask])
                do(P, 'dve', lambda e: e.memset(smask.t[:, :, 0:1], 0.0), writes=[smask])
                do(P, 'dve', lambda e: e.tensor_copy(out=totT.t[:], in_=tps.t[:].rearrange("p (j h) -> p h j", h=NH)),
                   reads=[tps], writes=[totT])
                do(P, 'dve', lambda e: e.tensor_tensor_scan(out=incl.t[:].rearrange("p h j -> p (h j)"),
                                                            data0=smask.t[:].rearrange("p h j -> p (h j)"),
                                                            data1=totT.t[:].rearrange("p h j -> p (h j)"),
                                                            initial=0.0, op0=ALU.mult, op1=ALU.add),
                   reads=[smask, totT], writes=[incl])
                do(P, 'dve', lambda e: e.tensor_copy(out=crefs.t[:].rearrange("p q h -> p h q"),
                                                     in_=incl.t[:].rearrange("p h (q f) -> p h q f", f=4)[:, :, :, 3]),
                   reads=[incl], writes=[crefs])
                do(P, 'dve', lambda e: e.tensor_tensor(out=totT.t[:], in0=incl.t[:], in1=totT.t[:], op=ALU.subtract),
                   reads=[incl, totT], writes=[totT])
                do(P, 'dve', lambda e: e.tensor_tensor(out=cumT.t[:].rearrange("p j h -> p h j"),
                                                       in0=cps.t[:].rearrange("p (j h) -> p h j", h=NH), in1=totT.t[:], op=ALU.add),
                   reads=[cps, totT], writes=[cumT])
                for qb in range(NB):
                    nj = 4 * qb + 4
                    do(P, 'dve', lambda e, qb=qb, nj=nj: e.tensor_tensor(
                        out=tab.t[:, qb, 0:nj, :], in0=cumT.t[:, 0:nj, :],
                        in1=crefs.t[:, qb:qb + 1, :].to_broadcast([128, nj, NH]), op=ALU.subtract),
                       reads=[cumT, crefs], writes=[tab])
                HB = max(NB // 2, 1)
                HW = HB * 512
                trp = SA.ps("trp", [NH, HW], F32)
                dd = SA.sb("dd", [NH, HW], F32)
                dcol = SA.sb("dcol", [NH, HB], F32)
                dhi = [SA.sb("dhi%d" % i, [NH, HW], BF16) for i in range(1)]
                dlo = [SA.sb("dlo%d" % i, [NH, HW], BF16) for i in range(1)]
                for hf in range(NB // HB):
                    for i in range(HB * 4):
                        do(P, 'pe', lambda e, i=i, hf=hf: e.transpose(out=trp.t[:, i * 128:(i + 1) * 128],
                                                                      in_=cumT.t[:, hf * HB * 4 + i, :], identity=c['ident_f'].t[:]),
                           reads=[cumT, c['ident_f']], writes=[trp], inc=(i == HB * 4 - 1))
                    hi_ = dhi[0]
                    lo_ = dlo[0]
                    do(P, 'dve', lambda e: e.tensor_copy(out=dcol.t[:], in_=trp.t[:].rearrange("h (q f) -> h q f", f=512)[:, :, 511]),
                       reads=[trp], writes=[dcol])
                    do(P, 'dve', lambda e: e.tensor_tensor(out=dd.t[:].rearrange("h (q f) -> h q f", f=512),
                                                           in0=trp.t[:].rearrange("h (q f) -> h q f", f=512),
                                                           in1=dcol.t[:].unsqueeze(2).to_broadcast([NH, HB, 512]), op=ALU.subtract),
                       reads=[trp, dcol], writes=[dd])
                    do(P, 'dve', lambda e: e.tensor_scalar(out=dd.t[:], in0=dd.t[:], scalar1=-8.0, scalar2=None, op0=ALU.mult),
                       reads=[dd], writes=[dd])
                    do(P, 'dve', lambda e, hi_=hi_: e.tensor_copy(out=hi_.t[:], in_=dd.t[:]), reads=[dd], writes=[hi_])
                    do(P, 'dve', lambda e, hi_=hi_, lo_=lo_: e.tensor_tensor(out=lo_.t[:], in0=dd.t[:], in1=hi_.t[:], op=ALU.subtract),
                       reads=[dd, hi_], writes=[lo_])
                    dma(P, 'sp', aug_d[:, 0, hf * HW:(hf + 1) * HW], hi_.t[:], reads=[hi_], owner=hi_)
                    dma(P, 'sp', aug_d[:, 1, hf * HW:(hf + 1) * HW], lo_.t[:], reads=[lo_], owner=lo_)
            SA.close()
            if stop_after == ('A', l):
                SAB.close()
                break

            SB_ = Scope(P, "b%d" % l)
            bg = cast_ffn_list(l)
            if l + 1 < DEPTH:
                bg += cast_wqkv_list(l + 1)
            if fox:
                strm = [(AttnStream(SB_, S, True, "a", 3, 0, 2), list(range(NH)))]
            else:
                strm = [(AttnStream(SB_, S, False, "a", 2, 1, 1), list(range(0, NH, 2))),
                        (AttnStream(SB_, S, False, "b", 2, 1, 1), list(range(1, NH, 2)))]
            if fox:
                for AB, _h in strm:
                    for sl in range(2):
                        do(P, 'dve', lambda e, sl=sl, AB=AB: e.memset(AB.kT[sl].t[64:66, :], 1.0), writes=[AB.kT[sl]])
                        do(P, 'pool', lambda e, sl=sl, AB=AB: e.memset(AB.VO[sl].t[:, :, 64:128], 1.0), writes=[AB.VO[sl]])

            def load_head(AB, h, sl, fox=fox, Vres=Vres):
                dma(P, 'sp', AB.qT[sl].t[0:64, :], qT_d[h * 64:(h + 1) * 64, :], writes=[AB.qT[sl]], owner=AB.qT[sl])
                dma(P, 'sp', AB.kT[sl].t[0:64, :], kT_d[h * 64:(h + 1) * 64, :], writes=[AB.kT[sl]], owner=AB.kT[sl])
                if fox:
                    dma(P, 'sp', AB.qT[sl].t[64:66, :], aug_d[h], writes=[AB.qT[sl]], owner=AB.qT[sl])
                    do(P, 'pool', lambda e: e.tensor_copy(out=AB.VO[sl].t[:, :, 0:64], in_=Vres.t[:, :, h * 64:(h + 1) * 64]),
                       reads=[Vres], writes=[AB.VO[sl]])

            def store_q(oq, h, qb):
                dma(P, 'sp', oT_d[h * 64:(h + 1) * 64, qb * QB:(qb + 1) * QB], oq.t[:], reads=[oq], owner=oq)

            def v_of(AB, h, sl, j, Vres=Vres, fox=fox):
                if fox:
                    return AB.VO[sl].t[:, j, :], AB.VO[sl]
                return Vres.t[:, j, h * 64:(h + 1) * 64], Vres

            bias_of = None
            if fox:
                def bias_of(h, qb, j, tab=tab):
                    return tab.t[:, qb, j, h:h + 1], tab
            emit_attention(P, strm, c, fox, load_head, store_q, v_of, bias_of, background=bg)
            SB_.close()
            SAB.close()
            if stop_after == ('B', l):
                break

            SC = Scope(P, "c%d" % l)
            Wo = SC.sb("Wo", [128, NDT, D], BF16)
            dma(P, 'sp', Wo.t[:], wob_d, reads=[wo_b], writes=[Wo], owner=Wo)
            Wd = SC.sb("Wd", [128, NFT, D], BF16)
            for q4 in range(0, NFT, 11):
                dma(P, 'sp', Wd.t[:, q4:q4 + 11, :], wdb_d[:, q4:q4 + 11, :], reads=[wd_b], writes=[Wd], owner=Wd)
            wcv = SC.sb("wcv", [128, NFT, 3], F32)
            bcv = SC.sb("bcv", [128, NFT], F32)
            dma(P, 'sp', wcv.t[:], wcv_d[l], writes=[wcv], owner=wcv)
            dma(P, 'sp', bcv.t[:], bcv_d[l], writes=[bcv], owner=bcv)
            GG = SC.sb("GG", [128, 2, D], F32)
            for which in range(2):
                dma(P, 'sp', GG.t[:, which, :], gg_d[l, which], writes=[GG], owner=GG)
            halo = SC.sb("halo", [128, NFT, 2], F32)
            do(P, 'dve', lambda e: e.memset(halo.t[:], 0.0), writes=[halo])
            oTb = SC.sb("oTb", [128, NDT, 512], BF16)
            xt = [SC.sb("xt%d" % i, [128, D], F32) for i in range(3)]
            xnw = [SC.sb("xnw%d" % i, [128, D], F32) for i in range(3)]
            xn = [SC.sb("xn%d" % i, [128, D], BF16) for i in range(2)]
            junk = SC.sb("junk", [128, D], BF16)
            t1 = [SC.sb("t1%d" % i, [128, D], F32) for i in range(2)]
            xo = [SC.sb("xo%d" % i, [128, D], F32) for i in range(2)]
            h2T = [SC.sb("h2T%d" % i, [128, NDT, 512], BF16) for i in range(2)]
            aT = SC.sb("aT", [128, NFT, 512], BF16)
            wgr = [SC.sb("wg%d" % i, [128, NDT, 128], BF16) for i in range(4)]
            wur = [SC.sb("wu%d" % i, [128, NDT, 128], BF16) for i in range(4)]
            gbuf = [SC.sb("gbuf%d" % i, [128, 514], F32) for i in range(2)]
            cv = [SC.sb("cv%d" % i, [128, 512], F32) for i in range(3)]
            sg = [SC.sb("sg%d" % i, [128, 512], F32) for i in range(3)]
            tp = SC.ps("tp", [128, NDT, 128], BF16)
            yps = [SC.ps("yps%d" % i, [128, D], F32) for i in range(2)]
            gps = [SC.ps("gps%d" % i, [128, 512], F32) for i in range(2)]
            ups = SC.ps("ups", [128, 512], F32)
            ydr = [Buf(P, None, None, "ydr%d" % j) for j in range(NT)]
            cnt = dict(y=0, t1=0, xt=0, xnw=0, xn=0, xo=0)
            Tstate = {}

            def T_stages(tb, i):
                j = tb * 4 + i
                hT_ = h2T[tb % 2]
                stt = {}

                def s0():
                    y_ = yps[cnt['y'] % 2]
                    cnt['y'] += 1
                    stt['y'] = y_
                    for hf in range(2):
                        for et in range(NDT):
                            do(P, 'pe', lambda e, et=et, hf=hf: e.matmul(
                                y_.t[:, hf * 512:(hf + 1) * 512], lhsT=oTb.t[:, et, i * 128:(i + 1) * 128],
                                rhs=Wo.t[:, et, hf * 512:(hf + 1) * 512], start=(et == 0), stop=(et == NDT - 1)),
                               reads=[oTb, Wo], writes=[y_], inc=(et == NDT - 1 and hf == 1))
                    x_ = xt[cnt['xt'] % 3]
                    cnt['xt'] += 1
                    stt['x'] = x_
                    dma(P, 'pool', x_.t[:], xsrc[j * 128:(j + 1) * 128, :], reads=[ydr[j]] if l > 0 else [], writes=[x_], owner=x_)

                def mk_stats(key_src, key_out):
                    def a():
                        src = stt[key_src]
                        sb_ = stat[statn[0] % len(stat)]
                        statn[0] += 1
                        stt[key_out] = sb_
                        do(P, 'act', lambda e: e.activation(out=junk.t[:], in_=src.t[:], func=AF.Square, accum_out=sb_.t[:, 0:1]),
                           reads=[src], writes=[junk, sb_])

                    def b():
                        sb_ = stt[key_out]
                        do(P, 'act', lambda e: e.activation(out=sb_.t[:, 1:2], in_=sb_.t[:, 0:1], func=AF.Ln, scale=1.0 / D, bias=EPS),
                           reads=[sb_], writes=[sb_])

                    def c_():
                        sb_ = stt[key_out]
                        do(P, 'act', lambda e: e.activation(out=sb_.t[:, 2:3], in_=sb_.t[:, 1:2], func=AF.Exp, scale=-0.5),
                           reads=[sb_], writes=[sb_])
                    return [a, b, c_]

                def s4():
                    t_ = t1[cnt['t1'] % 2]
                    cnt['t1'] += 1
                    stt['t'] = t_
                    y_ = stt['y']
                    sb_ = stt['st1']
                    do(P, 'act', lambda e: e.activation(out=t_.t[:], in_=y_.t[:], func=AF.Copy, scale=sb_.t[:, 2:3]),
                       reads=[y_, sb_], writes=[t_])

                def s5():
                    xw = xnw[cnt['xnw'] % 3]
                    cnt['xnw'] += 1
                    stt['xw'] = xw
                    t_ = stt['t']
                    x_ = stt['x']
                    do(P, 'pool', lambda e: e.tensor_tensor(out=t_.t[:], in0=t_.t[:], in1=GG.t[:, 0, :], op=ALU.mult),
                       reads=[GG], writes=[t_])
                    do(P, 'pool', lambda e: e.tensor_tensor(out=xw.t[:], in0=t_.t[:], in1=x_.t[:], op=ALU.add),
                       reads=[t_, x_], writes=[xw])
                    dma(P, 'pool', y_d[j * 128:(j + 1) * 128, :], xw.t[:], reads=[xw], writes=[ydr[j]], owner=xw)

                def s9():
                    xn_ = xn[cnt['xn'] % 2]
                    cnt['xn'] += 1
                    stt['xn'] = xn_
                    xw = stt['xw']
                    sb2 = stt['st2']
                    do(P, 'dve', lambda e: e.tensor_scalar(out=xn_.t[:], in0=xw.t[:], scalar1=sb2.t[:, 2:3], scalar2=None, op0=ALU.mult),
                       reads=[xw, sb2], writes=[xn_])

                def s10():
                    xn_ = stt['xn']
                    for dt in range(NDT):
                        do(P, 'pe', lambda e, dt=dt: e.transpose(out=tp.t[:, dt, :], in_=xn_.t[:, dt * 128:(dt + 1) * 128],
                                                                 identity=c['ident'].t[:]),
                           reads=[xn_, c['ident']], writes=[tp], inc=(dt == NDT - 1))

                def s11():
                    for dt in range(NDT):
                        do(P, 'dve', lambda e, dt=dt: e.tensor_scalar(out=hT_.t[:, dt, i * 128:(i + 1) * 128], in0=tp.t[:, dt, :],
                                                                      scalar1=abcol.t[:, l, 2, dt:dt + 1],
                                                                      scalar2=abcol.t[:, l, 3, dt:dt + 1],
                                                                      op0=ALU.mult, op1=ALU.add),
                           reads=[tp, abcol], writes=[hT_])
                return [s0] + mk_stats('y', 'st1') + [s4, s5] + mk_stats('xw', 'st2') + [s9, s10, s11]

            def load_oTb(tb):
                dma(P, 'sp', oTb.t[:], oT_d.rearrange("(et p) t -> p et t", p=128)[:, :, tb * 512:(tb + 1) * 512],
                    writes=[oTb], owner=oTb)

            nw = [0]

            def F1(tb):
                hT_ = h2T[tb % 2]
                sched = {}
                if tb + 1 < NB:
                    for i_ in range(4):
                        for k2, fn_ in enumerate(T_stages(tb + 1, i_)):
                            sched.setdefault(8 * i_ + k2, []).append(fn_)
                info = {}
                for it in range(NFT + 2):
                    if it < NFT:
                        ft = it
                        k_ = nw[0]
                        nw[0] += 1
                        wg_ = wgr[k_ % 4]
                        wu_ = wur[k_ % 4]
                        g_ = gps[k_ % 2]
                        gb = gbuf[k_ % 2]
                        cv_ = cv[k_ % 3]
                        sg_ = sg[k_ % 3]
                        info[ft] = (wu_, cv_, sg_)
                        dma(P, 'sp', wg_.t[:], wgb_d[ft], reads=[wg_b], writes=[wg_], owner=wg_)
                        dma(P, 'sp', wu_.t[:], wub_d[ft], reads=[wu_b], writes=[wu_], owner=wu_)
                        for dt in range(NDT):
                            do(P, 'pe', lambda e, dt=dt: e.matmul(g_.t[:], lhsT=wg_.t[:, dt, :], rhs=hT_.t[:, dt, :],
                                                                  start=(dt == 0), stop=(dt == NDT - 1)),
                               reads=[wg_, hT_], writes=[g_], inc=(dt == NDT - 1))
                        do(P, 'act', lambda e: e.activation(out=gb.t[:, 2:514], in_=g_.t[:], func=AF.Copy),
                           reads=[g_], writes=[gb])
                        do(P, 'act', lambda e: e.activation(out=cv_.t[:], in_=g_.t[:], func=AF.Identity,
                                                            scale=wcv.t[:, ft, 2:3], bias=bcv.t[:, ft:ft + 1]),
                           reads=[g_, wcv, bcv], writes=[cv_])
                        do(P, 'act', lambda e: e.activation(out=gb.t[:, 0:2], in_=halo.t[:, ft, :], func=AF.Copy),
                           reads=[halo], writes=[gb])
                        do(P, 'act', lambda e: e.activation(out=halo.t[:, ft, :], in_=g_.t[:, 510:512], func=AF.Copy),
                           reads=[g_], writes=[halo])
                        do(P, 'dve', lambda e: e.scalar_tensor_tensor(
                            out=cv_.t[:], in0=gb.t[:, 1:513], scalar=wcv.t[:, ft, 1:2], in1=cv_.t[:],
                            op0=ALU.mult, op1=ALU.add), reads=[gb, wcv, cv_], writes=[cv_])
                        do(P, 'dve', lambda e: e.scalar_tensor_tensor(
                            out=cv_.t[:], in0=gb.t[:, 0:512], scalar=wcv.t[:, ft, 0:1], in1=cv_.t[:],
                            op0=ALU.mult, op1=ALU.add), reads=[gb, wcv, cv_], writes=[cv_])
                    if 1 <= it <= NFT:
                        _wu, pcv, psg = info[it - 1]
                        do(P, 'act', lambda e: e.activation(out=psg.t[:], in_=pcv.t[:], func=AF.Exp, scale=-1.0),
                           reads=[pcv], writes=[psg])
                        do(P, 'act', lambda e: e.activation(out=psg.t[:], in_=psg.t[:], func=AF.Ln, bias=1.0),
                           reads=[psg], writes=[psg])
                        do(P, 'act', lambda e: e.activation(out=psg.t[:], in_=psg.t[:], func=AF.Exp, scale=-1.0),
                           reads=[psg], writes=[psg])
                        do(P, 'pool', lambda e: e.tensor_tensor(out=psg.t[:], in0=psg.t[:], in1=pcv.t[:], op=ALU.mult),
                           reads=[pcv], writes=[psg])
                    for fn_ in sched.pop(2 * it, []):
                        fn_()
                    if 2 <= it:
                        pft = it - 2
                        pwu, _cv, psg = info.pop(pft)
                        for dt in range(NDT):
                            do(P, 'pe', lambda e, dt=dt: e.matmul(ups.t[:], lhsT=pwu.t[:, dt, :], rhs=hT_.t[:, dt, :],
                                                                  start=(dt == 0), stop=(dt == NDT - 1)),
                               reads=[pwu, hT_], writes=[ups], inc=(dt == NDT - 1))
                        do(P, 'dve', lambda e: e.tensor_tensor(out=aT.t[:, pft, :], in0=ups.t[:], in1=psg.t[:], op=ALU.mult),
                           reads=[psg, ups], writes=[aT])
                    for fn_ in sched.pop(2 * it + 1, []):
                        fn_()
                assert not sched

            def F2(tb, i):
                j = tb * 4 + i
                y_ = yps[cnt['y'] % 2]
                cnt['y'] += 1
                for hf in range(2):
                    for ft in range(NFT):
                        do(P, 'pe', lambda e, ft=ft, hf=hf: e.matmul(
                            y_.t[:, hf * 512:(hf + 1) * 512], lhsT=aT.t[:, ft, i * 128:(i + 1) * 128],
                            rhs=Wd.t[:, ft, hf * 512:(hf + 1) * 512], start=(ft == 0), stop=(ft == NFT - 1)),
                           reads=[aT, Wd], writes=[y_], inc=(ft == NFT - 1 and hf == 1))
                x_ = xt[cnt['xt'] % 3]
                cnt['xt'] += 1
                dma(P, 'pool', x_.t[:], y_d[j * 128:(j + 1) * 128, :], reads=[ydr[j]], writes=[x_], owner=x_)
                rstd, sb_ = rms_stats(y_.t[:], y_, junk, 1.0 / D)
                t_ = t1[cnt['t1'] % 2]
                cnt['t1'] += 1
                do(P, 'dve', lambda e: e.scalar_tensor_tensor(out=t_.t[:], in0=y_.t[:], scalar=rstd, in1=GG.t[:, 1, :],
                                                              op0=ALU.mult, op1=ALU.mult),
                   reads=[y_, sb_, GG], writes=[t_])
                xo_ = xo[cnt['xo'] % 2]
                cnt['xo'] += 1
                do(P, 'pool', lambda e: e.tensor_tensor(out=xo_.t[:], in0=t_.t[:], in1=x_.t[:], op=ALU.add),
                   reads=[t_, x_], writes=[xo_])
                dma(P, 'pool', y_d[j * 128:(j + 1) * 128, :], xo_.t[:], reads=[xo_], writes=[ydr[j]], owner=xo_)

            load_oTb(0)
            wavefront([T_stages(0, i) for i in range(4)], 3)
            for tb in range(NB):
                if tb + 1 < NB:
                    load_oTb(tb + 1)
                F1(tb)
                for i in range(4):
                    F2(tb, i)
            SC.close()
        G.close()
    return nc, P


def make_consts():
    kp = np.arange(128)[:, None]
    qf = np.arange(128)[None, :]
    ident = np.eye(128, dtype=np.float32)
    msk_sb = np.where(kp < qf, 0.0, NEG).astype(np.float32)
    msk_fox = np.where(kp <= qf, 0.0, NEG).astype(np.float32)
    Linc = (kp >= qf).astype(np.float32)
    Lcomp = (1.0 - Linc).astype(np.float32)
    U = (kp <= qf).astype(np.float32)
    ones = np.ones((128, 128), np.float32)
    return np.stack([ident, msk_sb, msk_fox, Linc, Lcomp, U, ones]).astype(np.float32)


def layout_inputs(inp, S, DEPTH):
    f = lambda a: np.ascontiguousarray(np.asarray(a, dtype=np.float32))
    L = DEPTH
    NF = max(DEPTH // 2, 1)
    shared = {}
    wm_ = np.asarray(inp["w_mod"])[:L]
    shared["w_mod"] = f(wm_.reshape(L, NDT, 128, 6 * D).transpose(0, 2, 1, 3))
    ng = np.concatenate([wm_[:, :, 0:2 * D], wm_[:, :, 3 * D:5 * D]], axis=2)
    shared["w_modT"] = f(ng.transpose(0, 2, 1).reshape(L, 32, 128, D).transpose(0, 2, 1, 3))
    bm = np.asarray(inp["b_mod"])[:L]
    shared["b_mod_col"] = f(bm.reshape(L, 48, 128).transpose(0, 2, 1))
    shared["b_mod"] = f(bm)
    gpre = np.stack([np.asarray(inp["g_mix_pre"])[:L], np.asarray(inp["g_ffn_pre"])[:L]], axis=1)
    shared["g_pre"] = f(gpre.reshape(L, 2, NDT, 128).transpose(0, 3, 1, 2))
    shared["g_post"] = f(np.stack([np.asarray(inp["g_mix_post"])[:L], np.asarray(inp["g_ffn_post"])[:L]], axis=1))
    shared["w_qkv"] = f(np.asarray(inp["w_qkv"])[:L].reshape(L, NDT, 128, 3 * D).transpose(0, 2, 1, 3))
    shared["w_o"] = f(np.asarray(inp["w_o"])[:L].reshape(L, NDT, 128, D).transpose(0, 2, 1, 3))
    wfg = np.asarray(inp["w_fg"])
    bfg = np.asarray(inp["b_fg"])
    shared["w_fg"] = f(wfg[:NF].reshape(NF, NDT, 128, NH).transpose(0, 2, 1, 3))
    shared["b_fg"] = f(bfg[:NF])
    for k, nm in (("w_ffn_gate", "w_g"), ("w_ffn_up", "w_u")):
        shared[nm] = f(np.asarray(inp[k])[:L].reshape(L, NDT, 128, NFT, 128).transpose(0, 3, 2, 1, 4))
    shared["w_d"] = f(np.asarray(inp["w_ffn_down"])[:L].reshape(L, NFT, 128, D).transpose(0, 2, 1, 3))
    shared["w_conv"] = f(np.asarray(inp["w_conv"])[:L].reshape(L, 3, NFT, 128).transpose(0, 3, 2, 1))
    shared["b_conv"] = f(np.asarray(inp["b_conv"])[:L].reshape(L, NFT, 128).transpose(0, 2, 1))
    shared["consts"] = make_consts()
    x = np.asarray(inp["x"])
    cc = np.asarray(inp["c"])
    maps = []
    for b in range(x.shape[0]):
        m = dict(shared)
        m["x"] = f(x[b, :S])
        m["c"] = f(cc[b].reshape(NDT, 128).T)
        m["c_row"] = f(cc[b].reshape(1, D))
        maps.append(m)
    return maps


def kernel(**inputs):
    x = np.asarray(inputs["x"])
    B, S, _ = x.shape
    DEPTH = np.asarray(inputs["w_qkv"]).shape[0]
    nc, _ = build_program(S, DEPTH)
    maps = layout_inputs(inputs, S, DEPTH)
    res = run_bass_kernel_spmd(nc, maps, core_ids=list(range(B)))
    out = np.stack([np.asarray(res.results[b]["y"], dtype=np.float32) for b in range(B)], axis=0)
    return out
```

```python
import numpy as np
import ml_dtypes
from contextlib import ExitStack
import concourse.bass as bass
import concourse.mybir as mybir
from concourse.bass_utils import run_bass_kernel_spmd

F32 = mybir.dt.float32
BF16 = mybir.dt.bfloat16
AF = mybir.ActivationFunctionType
ALU = mybir.AluOpType
AX = mybir.AxisListType

D = 1024
NH = 16
DH = 64
FF = 2816
NFT = FF // 128
NDT = D // 128
EPS = 1e-6
QB = 512
NEG = -30000.0


class Sem:
    _n = 0

    def __init__(self, h):
        self.h = h
        Sem._n += 1
        self.key = Sem._n


class Prog:
    ENG = ['pe', 'act', 'dve', 'pool', 'sp']

    def __init__(self, nc, es):
        self.nc = nc
        self.eng = {'pe': nc.tensor, 'act': nc.scalar, 'dve': nc.vector, 'pool': nc.gpsimd, 'sp': nc.sync}
        self.sem = {n: Sem(es.enter_context(nc.semaphore("s_" + n))) for n in self.ENG}
        self.cnt = {n: 0 for n in self.ENG}
        self.seen = {n: {} for n in self.ENG}
        self.streams = []
        self.free_streams = {'sw': [], 'hw': []}
        self.es = es
        self.bar = Sem(es.enter_context(nc.semaphore("s_bar")))
        self.barcnt = 0
        self.nins = 0
        self._uid = 0

    def uid(self):
        self._uid += 1
        return self._uid

    def waits(self, qn, ws):
        seen = self.seen[qn]
        e = self.eng[qn]
        for w in ws:
            if w is None:
                continue
            s, v = w
            if v <= 0 or seen.get(s.key, 0) >= v:
                continue
            seen[s.key] = v
            e.wait_ge(s.h, v)
            self.nins += 1

    def op(self, qn, fn, ws=(), inc=True):
        self.waits(qn, ws)
        ins = fn(self.eng[qn])
        self.nins += 1
        if inc:
            ins.then_inc(self.sem[qn].h, 1)
            self.cnt[qn] += 1
            return (self.sem[qn], self.cnt[qn])
        return None

    def dma(self, qn, out, in_, st, ws=()):
        self.waits(qn, ws)
        st.cnt += 16
        self.eng[qn].dma_start(out=out, in_=in_).then_inc(st.sem.h, 16)
        self.nins += 1
        return (st.sem, st.cnt)

    def barrier(self):
        ws = [(self.sem[n], self.cnt[n]) for n in self.ENG]
        ws += [(st.sem, st.cnt) for st in self.streams if st.cnt > 0]
        self.waits('sp', ws)
        self.barcnt += 16
        self.eng['sp'].dma_start(out=self.bar_dst, in_=self.bar_src).then_inc(self.bar.h, 16)
        for n in self.ENG:
            self.waits(n, [(self.bar, self.barcnt)])

    def get_stream(self, kind):
        if self.free_streams[kind]:
            return self.free_streams[kind].pop()
        st = DmaStream(self, "d%s%d" % (kind, self.uid()))
        st.kind = kind
        return st


class DmaStream:
    def __init__(self, P, name):
        self.sem = Sem(P.es.enter_context(P.nc.semaphore(name)))
        self.cnt = 0
        P.streams.append(self)


class Buf:
    def __init__(self, P, es, t=None, name=None):
        self.P = P
        self.es = es
        self.t = t
        self.w = None
        self.r = {}
        self.name = name or ("b%d" % P.uid())
        self._ds = {}

    def ds(self, kind):
        if kind not in self._ds:
            self._ds[kind] = self.P.get_stream(kind)
        return self._ds[kind]


class Scope:
    def __init__(self, P, tag):
        self.P = P
        self.es = ExitStack()
        self.tag = tag
        self.bufs = []

    def sb(self, name, shape, dt):
        t = self.es.enter_context(self.P.nc.sbuf_tensor("%s_%s_%d" % (self.tag, name, self.P.uid()), shape, dt))
        b = Buf(self.P, self.es, t, name)
        self.bufs.append(b)
        return b

    def ps(self, name, shape, dt):
        t = self.es.enter_context(self.P.nc.psum_tensor("%s_%s_%d" % (self.tag, name, self.P.uid()), shape, dt))
        b = Buf(self.P, self.es, t, name)
        self.bufs.append(b)
        return b

    def close(self):
        self.P.barrier()
        for b in self.bufs:
            for kind, st in b._ds.items():
                self.P.free_streams[kind].append(st)
            b._ds = {}
        self.es.close()


def _rw(reads, writes):
    ws = []
    for b in reads:
        ws.append(b.w)
    for b in writes:
        ws.append(b.w)
        ws.extend(b.r.values())
    return ws


def _upd(tok, reads, writes):
    for b in reads:
        if b in writes:
            continue
        o = b.r.get(tok[0].key)
        if o is None or o[1] < tok[1]:
            b.r[tok[0].key] = tok
    for b in writes:
        b.w = tok
        b.r = {}


def wavefront(stage_lists, stride):
    sched = {}
    for i, st in enumerate(stage_lists):
        for k, fn in enumerate(st):
            sched.setdefault(stride * i + k, []).append(fn)
    for slot in sorted(sched):
        for fn in sched[slot]:
            fn()


def do(P, qn, fn, reads=(), writes=(), inc=True):
    ws = _rw(reads, writes)
    if qn == 'pe':
        pes = P.sem['pe']
        ws = [w for w in ws if w is not None and w[0] is not pes]
    tok = P.op(qn, fn, ws, inc=inc)
    if tok is not None:
        _upd(tok, reads, writes)
    return tok


def dma(P, qn, out_ap, in_ap, reads=(), writes=(), owner=None):
    ws = _rw(reads, writes)
    tok = P.dma(qn, out_ap, in_ap, owner.ds('sw' if qn == 'pool' else 'hw'), ws)
    _upd(tok, reads, writes)
    return tok


class AttnStream:
    def __init__(self, sc, S, fox, tag, nz, nC, no):
        self.S = S
        self.z = [sc.ps("z%s%d" % (tag, i), [128, 512], F32) for i in range(nz)]
        self.C = [sc.ps("C%s%d" % (tag, i), [128, 512], F32) for i in range(nC)] if not fox else []
        self.o = [sc.ps("o%s%d" % (tag, i), [128, 512], F32) for i in range(no)]
        if not fox:
            self.e = [sc.sb("e%s%d" % (tag, i), [128, 512], F32) for i in range(3)]
            self.sp = [sc.sb("sp%s%d" % (tag, i), [128, 512], BF16) for i in range(3)]
            self.E2 = [sc.sb("E2%s%d" % (tag, i), [128, 512], F32) for i in range(2)]
        else:
            self.rden = [sc.sb("rden%s%d" % (tag, i), [128, 512], F32) for i in range(2)]
            self.VO = [sc.sb("VO%s%d" % (tag, i), [128, S // 128, 128], BF16) for i in range(2)]
        self.A = [sc.sb("A%s%d" % (tag, i), [128, 512], BF16) for i in range(3)]
        self.qT = [sc.sb("qT%s%d" % (tag, i), [66, S], BF16) for i in range(2)]
        self.kT = [sc.sb("kT%s%d" % (tag, i), [66, S], BF16) for i in range(2)]
        self.oTq = [sc.sb("oTq%s%d" % (tag, i), [64, 512], BF16) for i in range(2)]


def emit_attention(P, streams, c, fox, load_head, store_q, v_of, bias_of=None, background=()):
    S = streams[0][0].S
    nqb = S // QB
    tpq = QB // 128
    msk = c['msk_fox'] if fox else c['msk_sb']
    KD = 66 if fox else 64

    class Ctx:
        pass

    ctxs = []
    for si, (AB, heads) in enumerate(streams):
        cx = Ctx()
        cx.AB = AB
        cx.si = si
        cx.heads = heads
        cx.steps = []
        for hi, h in enumerate(heads):
            for qb in range(nqb):
                jd = (qb + 1) * tpq - 1
                for j in range(jd, -1, -1):
                    m = j - qb * tpq
                    cx.steps.append(dict(hi=hi, h=h, qb=qb, j=j, c0=(128 * m if m >= 0 else 0), diag=(m >= 0),
                                         first=(j == jd), last=(j == 0), qbg=hi * nqb + qb))
        cx.n = len(cx.steps)
        cx.loaded = set()
        ctxs.append(cx)

    def maybe_load(cx, hi):
        if hi < len(cx.heads) and hi not in cx.loaded:
            cx.loaded.add(hi)
            load_head(cx.AB, cx.heads[hi], hi % 2)

    def QK(cx, s):
        AB = cx.AB
        st = cx.steps[s]
        sl = st['hi'] % 2
        z = AB.z[s % len(AB.z)]
        c0 = st['c0']
        q0 = st['qb'] * QB
        j = st['j']
        qT = AB.qT[sl]
        kT = AB.kT[sl]
        do(P, 'pe', lambda e: e.matmul(z.t[:, c0:QB], lhsT=kT.t[0:KD, j * 128:(j + 1) * 128],
                                       rhs=qT.t[0:KD, q0 + c0:q0 + QB],
                                       start=True, stop=True, skip_group_check=True),
           reads=[qT, kT] + ([c['ident'], msk] if st['diag'] else []), writes=[z], inc=not st['diag'])
        if st['diag']:
            do(P, 'pe', lambda e: e.matmul(z.t[:, c0:c0 + 128], lhsT=c['ident'].t[:], rhs=msk.t[:],
                                           start=False, stop=True, skip_group_check=True),
               reads=[qT, kT, c['ident'], msk], writes=[z])

    def PV(cx, s):
        AB = cx.AB
        st = cx.steps[s]
        c0 = st['c0']
        j = st['j']
        o = AB.o[st['qbg'] % len(AB.o)]
        A = AB.A[s % 3]
        vap, vbuf = v_of(AB, st['h'], st['hi'] % 2, j)
        if not fox:
            do(P, 'pe', lambda e: e.matmul(o.t[0:64, c0:QB], lhsT=vap, rhs=A.t[:, c0:QB],
                                           start=st['first'], stop=st['last'], skip_group_check=True),
               reads=[vbuf, A], writes=[o])
        else:
            do(P, 'pe', lambda e: e.matmul(o.t[:, c0:QB], lhsT=vap, rhs=A.t[:, c0:QB],
                                           start=st['first'], stop=st['last'], skip_group_check=True),
               reads=[vbuf, A], writes=[o])

    def EVAC(cx, s):
        AB = cx.AB
        st = cx.steps[s]
        o = AB.o[st['qbg'] % len(AB.o)]
        oq = AB.oTq[st['qbg'] % 2]
        if not fox:
            do(P, 'dve', lambda e: e.tensor_copy(out=oq.t[:], in_=o.t[0:64, :]), reads=[o], writes=[oq])
        else:
            rd = AB.rden[st['qbg'] % 2]
            do(P, 'dve', lambda e: e.reciprocal(out=rd.t[64:128, :], in_=o.t[64:128, :]), reads=[o], writes=[rd])
            do(P, 'dve', lambda e: e.tensor_tensor(out=oq.t[:], in0=o.t[0:64, :], in1=rd.t[64:128, :], op=ALU.mult),
               reads=[o, rd], writes=[oq])
        store_q(oq, st['h'], st['qb'])

    def tick(cx, t):
        AB = cx.AB
        steps = cx.steps
        n = cx.n
        if not fox:
            if 0 <= t - 1 < n and not steps[t - 1]['last']:
                s = t - 1
                st = steps[s]
                C = AB.C[st['qbg'] % len(AB.C)]
                sp = AB.sp[s % 3]
                c0 = st['c0']
                do(P, 'pe', lambda e: e.matmul(C.t[:, c0:QB], lhsT=c['Lcomp'].t[:], rhs=sp.t[:, c0:QB],
                                               start=False, stop=True, skip_group_check=True),
                   reads=[sp, c['Lcomp']], writes=[C])
            if 0 <= t < n:
                s = t
                st = steps[s]
                C = AB.C[st['qbg'] % len(AB.C)]
                sp = AB.sp[s % 3]
                c0 = st['c0']
                do(P, 'pe', lambda e: e.matmul(C.t[:, c0:QB], lhsT=c['Linc'].t[:], rhs=sp.t[:, c0:QB],
                                               start=st['first'], stop=True, skip_group_check=True),
                   reads=[sp, c['Linc']], writes=[C])
        if 0 <= t - 1 < n:
            PV(cx, t - 1)
        if 0 <= t + 2 < n:
            QK(cx, t + 2)
        if not fox:
            if 0 <= t + 1 < n:
                s = t + 1
                st = steps[s]
                z = AB.z[s % len(AB.z)]
                ee = AB.e[s % 3]
                c0 = st['c0']
                do(P, 'act', lambda e: e.activation(out=ee.t[:, c0:QB], in_=z.t[:, c0:QB], func=AF.Exp, scale=0.125),
                   reads=[z], writes=[ee])
            if 0 <= t < n:
                s = t
                st = steps[s]
                C = AB.C[st['qbg'] % len(AB.C)]
                E2 = AB.E2[s % 2]
                c0 = st['c0']
                do(P, 'act', lambda e: e.activation(out=E2.t[:, c0:QB], in_=C.t[:, c0:QB], func=AF.Exp, scale=-1.0),
                   reads=[C], writes=[E2])
            if 0 <= t + 1 < n:
                s = t + 1
                st = steps[s]
                ee = AB.e[s % 3]
                sp = AB.sp[s % 3]
                c0 = st['c0']
                do(P, 'act', lambda e: e.activation(out=sp.t[:, c0:QB], in_=ee.t[:, c0:QB], func=AF.Ln, bias=1.0, scale=1.0),
                   reads=[ee], writes=[sp])
        else:
            if 0 <= t + 1 < n:
                s = t + 1
                st = steps[s]
                z = AB.z[s % len(AB.z)]
                A = AB.A[s % 3]
                c0 = st['c0']
                bap, bbuf = bias_of(st['h'], st['qb'], st['j'])
                do(P, 'act', lambda e: e.activation(out=A.t[:, c0:QB], in_=z.t[:, c0:QB], func=AF.Exp, scale=0.125, bias=bap),
                   reads=[z, bbuf], writes=[A])
        if 0 <= t - 1 < n and steps[t - 1]['last']:
            EVAC(cx, t - 1)
        if not fox and 0 <= t < n:
            s = t
            st = steps[s]
            ee = AB.e[s % 3]
            E2 = AB.E2[s % 2]
            A = AB.A[s % 3]
            c0 = st['c0']
            do(P, 'dve', lambda e: e.tensor_tensor(out=A.t[:, c0:QB], in0=ee.t[:, c0:QB], in1=E2.t[:, c0:QB], op=ALU.mult),
               reads=[ee, E2], writes=[A])
        if 0 <= t < n and (t == 0 or steps[t]['hi'] != steps[t - 1]['hi']):
            maybe_load(cx, steps[t]['hi'] + 1)

    for cx in ctxs:
        maybe_load(cx, 0)
    nmax = max(cx.n for cx in ctxs)
    background = list(background)
    for t in range(-2, nmax + 2):
        for cx in ctxs:
            tick(cx, t)
        if background and t >= 4 and t % 4 == 0:
            background.pop(0)()
    for fn in background:
        fn()


CONST_NAMES = ['ident', 'msk_sb', 'msk_fox', 'Linc', 'Lcomp', 'U', 'ones']


def build_program(S, DEPTH, stop_after=None):
    NT = S // 128
    NB = S // QB
    NF = DEPTH // 2
    nc = bass.Bass("TRN2", target_bir_lowering=False)

    def din(name, shape, dt=F32):
        return nc.dram_tensor(name, shape, dt, kind="ExternalInput").ap()

    x_d = din("x", [S, D])
    c_d = din("c", [128, NDT])
    wmod_d = din("w_mod", [DEPTH, 128, NDT, 6 * D])
    wmodT_d = din("w_modT", [DEPTH, 128, 32, D])
    crow_d = din("c_row", [1, D])
    bmodc_d = din("b_mod_col", [DEPTH, 128, 48])
    bmod_d = din("b_mod", [DEPTH, 6 * D])
    gpre_d = din("g_pre", [DEPTH, 128, 2, NDT])
    gpost_d = din("g_post", [DEPTH, 2, D])
    wqkv_d = din("w_qkv", [DEPTH, 128, NDT, 3 * D])
    wo_d = din("w_o", [DEPTH, 128, NDT, D])
    wfg_d = din("w_fg", [max(NF, 1), 128, NDT, NH])
    bfg_d = din("b_fg", [max(NF, 1), NH])
    wg_d = din("w_g", [DEPTH, NFT, 128, NDT, 128])
    wu_d = din("w_u", [DEPTH, NFT, 128, NDT, 128])
    wd_d = din("w_d", [DEPTH, 128, NFT, D])
    wcv_d = din("w_conv", [DEPTH, 128, NFT, 3])
    bcv_d = din("b_conv", [DEPTH, 128, NFT])
    cst_d = din("consts", [len(CONST_NAMES), 128, 128])
    y_d = nc.dram_tensor("y", [S, D], F32, kind="ExternalOutput").ap()
    qT_d = nc.dram_tensor("qT_s", [D, S], BF16, kind="Internal").ap()
    kT_d = nc.dram_tensor("kT_s", [D, S], BF16, kind="Internal").ap()
    oT_d = nc.dram_tensor("oT_s", [D, S], BF16, kind="Internal").ap()
    aug_d = nc.dram_tensor("aug_s", [NH, 2, S], BF16, kind="Internal").ap()
    gg_d = nc.dram_tensor("gg_s", [DEPTH, 2, 128, D], F32, kind="Internal").ap()
    bar_d = nc.dram_tensor("bar_s", [2, 16], F32, kind="Internal").ap()
    wqkvb_d = nc.dram_tensor("wqkvb_s", [128, NDT, 3 * D], BF16, kind="Internal").ap()
    wob_d = nc.dram_tensor("wob_s", [128, NDT, D], BF16, kind="Internal").ap()
    wgb_d = nc.dram_tensor("wgb_s", [NFT, 128, NDT, 128], BF16, kind="Internal").ap()
    wub_d = nc.dram_tensor("wub_s", [NFT, 128, NDT, 128], BF16, kind="Internal").ap()
    wdb_d = nc.dram_tensor("wdb_s", [128, NFT, D], BF16, kind="Internal").ap()

    es = ExitStack()
    with es:
        P = Prog(nc, es)
        P.bar_dst = bar_d[0:1, :]
        P.bar_src = cst_d[0, 0:1, 0:16]
        G = Scope(P, "g")
        c = {}
        for i, nm in enumerate(['ident', 'msk_sb', 'msk_fox', 'Linc', 'Lcomp']):
            c[nm] = G.sb("c_" + nm, [128, 128], BF16)
            dma(P, 'pool', c[nm].t[:], cst_d[i], writes=[c[nm]], owner=c[nm])
        for i, nm in [(0, 'ident_f'), (5, 'U_f'), (6, 'ones_f')]:
            c[nm] = G.sb("c_" + nm, [128, 128], F32)
            dma(P, 'sp', c[nm].t[:], cst_d[i], writes=[c[nm]], owner=c[nm])
        c['ones64'] = G.sb("ones64", [128, 64], BF16)
        do(P, 'dve', lambda e: e.memset(c['ones64'].t[:], 1.0), writes=[c['ones64']])
        neghalf = G.sb("neghalf", [128, 1], F32)
        do(P, 'dve', lambda e: e.memset(neghalf.t[:], -0.5), writes=[neghalf])
        wq_b = Buf(P, None, None, "wq_b")
        wo_b = Buf(P, None, None, "wo_b")
        wg_b = Buf(P, None, None, "wg_b")
        wu_b = Buf(P, None, None, "wu_b")
        wd_b = Buf(P, None, None, "wd_b")
        G.bufs += [wq_b, wo_b, wg_b, wu_b, wd_b]

        def cast_wqkv_list(l_):
            return [lambda dt=dt: dma(P, 'pool', wqkvb_d[:, dt, :], wqkv_d[l_, :, dt, :], writes=[wq_b], owner=wq_b)
                    for dt in range(NDT)]

        def cast_wqkv(l_):
            for fn in cast_wqkv_list(l_):
                fn()

        def cast_ffn_list(l_):
            fl_ = [lambda: dma(P, 'pool', wob_d, wo_d[l_], writes=[wo_b], owner=wo_b)]
            for q4 in range(0, NFT, 2):
                fl_.append(lambda q4=q4: dma(P, 'pool', wdb_d[:, q4:q4 + 2, :], wd_d[l_, :, q4:q4 + 2, :], writes=[wd_b], owner=wd_b))
            for q4 in range(0, NFT, 2):
                fl_.append(lambda q4=q4: dma(P, 'pool', wgb_d[q4:q4 + 2], wg_d[l_, q4:q4 + 2], writes=[wg_b], owner=wg_b))
                fl_.append(lambda q4=q4: dma(P, 'pool', wub_d[q4:q4 + 2], wu_d[l_, q4:q4 + 2], writes=[wu_b], owner=wu_b))
            return fl_

        cast_wqkv(0)
        abcol = G.sb("abcol", [128, DEPTH, 4, NDT], F32)
        stat = [G.sb("stat%d" % i, [128, 4], F32) for i in range(8)]
        statn = [0]

        def rms_stats(src_ap, srcbuf, junk, width_scale):
            sb_ = stat[statn[0] % len(stat)]
            statn[0] += 1
            do(P, 'act', lambda e: e.activation(out=junk.t[:], in_=src_ap, func=AF.Square, accum_out=sb_.t[:, 0:1]),
               reads=[srcbuf], writes=[junk, sb_])
            do(P, 'act', lambda e: e.activation(out=sb_.t[:, 1:2], in_=sb_.t[:, 0:1], func=AF.Ln, scale=width_scale, bias=EPS),
               reads=[sb_], writes=[sb_])
            do(P, 'act', lambda e: e.activation(out=sb_.t[:, 2:3], in_=sb_.t[:, 1:2], func=AF.Exp, scale=-0.5),
               reads=[sb_], writes=[sb_])
            return sb_.t[:, 2:3], sb_

        PR = Scope(P, "pr")
        cT = PR.sb("cT", [128, NDT], F32)
        cact = PR.sb("cact", [128, NDT], F32)
        crep = PR.sb("crep", [128, NDT, 128], F32)
        dma(P, 'sp', cT.t[:], c_d, writes=[cT], owner=cT)
        do(P, 'act', lambda e: e.activation(out=cact.t[:], in_=cT.t[:], func=AF.Silu), reads=[cT], writes=[cact])
        for dt in range(NDT):
            do(P, 'dve', lambda e, dt=dt: e.tensor_scalar(out=crep.t[:, dt, :], in0=c['ones_f'].t[:], scalar1=cact.t[:, dt:dt + 1],
                                                          scalar2=None, op0=ALU.mult), reads=[cact, c['ones_f']], writes=[crep])
        wm = [PR.sb("wm%d" % i, [128, NDT, 512], F32) for i in range(2)]
        wmT = [PR.sb("wmT%d" % i, [128, 4, D], F32) for i in range(3)]
        cbc = PR.sb("cbc", [128, D], F32)
        junkf = PR.sb("junkf", [128, D], F32)
        dma(P, 'sp', cbc.t[:], crow_d[0, :].partition_broadcast(128), writes=[cbc], owner=cbc)
        do(P, 'act', lambda e: e.activation(out=cbc.t[:], in_=cbc.t[:], func=AF.Silu), reads=[cbc], writes=[cbc])
        modcol = PR.sb("modcol", [128, 48], F32)
        nchT = 0
        ggps = [PR.ps("ggps%d" % i, [128, 512], F32) for i in range(2)]
        bmodc = PR.sb("bmodc", [128, 48], F32)
        modc = PR.sb("modc", [128, 48], F32)
        gpre = PR.sb("gpre", [128, 2, NDT], F32)
        bmbc = PR.sb("bmbc", [128, 2, D], F32)
        gpbc = PR.sb("gpbc", [128, 2, D], F32)
        ggst = [PR.sb("ggst%d" % i, [128, 512], F32) for i in range(2)]
        nchunk = 0
        ngg = 0
        for l in range(DEPTH):
            dma(P, 'sp', bmodc.t[:], bmodc_d[l], writes=[bmodc], owner=bmodc)
            dma(P, 'sp', gpre.t[:], gpre_d[l], writes=[gpre], owner=gpre)
            for which, v in enumerate((2, 5)):
                dma(P, 'sp', bmbc.t[:, which, :], bmod_d[l, v * D:(v + 1) * D].partition_broadcast(128),
                    writes=[bmbc], owner=bmbc)
                dma(P, 'sp', gpbc.t[:, which, :], gpost_d[l, which, :].partition_broadcast(128),
                    writes=[gpbc], owner=gpbc)
            for mcb in range(12):
                v = mcb // 2
                half = mcb % 2
                if v in (2, 5):
                    w_ = wm[nchunk % 2]
                    nchunk += 1
                    dma(P, 'pool', w_.t[:], wmod_d[l, :, :, mcb * 512:(mcb + 1) * 512], writes=[w_], owner=w_)
                    which = 0 if v == 2 else 1
                    gp = ggps[ngg % 2]
                    gs = ggst[ngg % 2]
                    ngg += 1
                    for dt in range(NDT):
                        do(P, 'pe', lambda e, dt=dt, gp=gp, w_=w_: e.matmul(gp.t[:], lhsT=crep.t[:, dt, :], rhs=w_.t[:, dt, :],
                                                                            start=(dt == 0), stop=(dt == NDT - 1)),
                           reads=[crep, w_], writes=[gp], inc=(dt == NDT - 1))
                    do(P, 'dve', lambda e, gp=gp, gs=gs, which=which, half=half: e.tensor_tensor(
                        out=gs.t[:], in0=gp.t[:], in1=bmbc.t[:, which, half * 512:(half + 1) * 512], op=ALU.add),
                       reads=[gp, bmbc], writes=[gs])
                    do(P, 'dve', lambda e, gs=gs, which=which, half=half: e.tensor_tensor(
                        out=gs.t[:], in0=gs.t[:], in1=gpbc.t[:, which, half * 512:(half + 1) * 512], op=ALU.mult),
                       reads=[gs, gpbc], writes=[gs])
                    dma(P, 'sp', gg_d[l, which, :, half * 512:(half + 1) * 512], gs.t[:], reads=[gs], owner=gs)
                else:
                    kq = mcb * 4 if mcb < 4 else mcb * 4 - 8
                    wt = wmT[nchT % 3]
                    nchT += 1
                    dma(P, 'sp', wt.t[:], wmodT_d[l, :, kq:kq + 4, :], writes=[wt], owner=wt)
                    for sc_ in range(4):
                        mc = mcb * 4 + sc_
                        do(P, 'dve', lambda e, mc=mc, sc_=sc_, wt=wt: e.scalar_tensor_tensor(
                            out=junkf.t[:], in0=wt.t[:, sc_, :], scalar=1.0, in1=cbc.t[:], op0=ALU.mult, op1=ALU.mult,
                            accum_out=modcol.t[:, mc:mc + 1]), reads=[wt, cbc], writes=[junkf, modcol])
            for (a0, a1) in ((0, 16), (24, 40)):
                do(P, 'dve', lambda e, a0=a0, a1=a1: e.tensor_tensor(out=modc.t[:, a0:a1], in0=modcol.t[:, a0:a1],
                                                                     in1=bmodc.t[:, a0:a1], op=ALU.add),
                   reads=[modcol, bmodc], writes=[modc])
            for which, (sh, scl) in enumerate(((0, 1), (3, 4))):
                do(P, 'dve', lambda e, which=which, scl=scl, l=l: e.scalar_tensor_tensor(
                    out=abcol.t[:, l, 2 * which, :], in0=modc.t[:, scl * 8:(scl + 1) * 8], scalar=1.0, in1=gpre.t[:, which, :],
                    op0=ALU.add, op1=ALU.mult), reads=[modc, gpre], writes=[abcol])
                do(P, 'dve', lambda e, which=which, sh=sh, l=l: e.tensor_copy(
                    out=abcol.t[:, l, 2 * which + 1, :], in_=modc.t[:, sh * 8:(sh + 1) * 8]), reads=[modc], writes=[abcol])
        PR.close()
        if stop_after == ('P',):
            G.close()
            return nc, P

        def norm_transpose(sc, src_ap, srcbuf, xn, junk, tp, hT, col0, l, which):
            rstd, sb_ = rms_stats(src_ap, srcbuf, junk, 1.0 / D)
            do(P, 'dve', lambda e: e.tensor_scalar(out=xn.t[:], in0=src_ap, scalar1=rstd, scalar2=None, op0=ALU.mult),
               reads=[srcbuf, sb_], writes=[xn])
            for dt in range(NDT):
                do(P, 'pe', lambda e, dt=dt: e.transpose(out=tp.t[:, dt, :], in_=xn.t[:, dt * 128:(dt + 1) * 128],
                                                         identity=c['ident'].t[:]),
                   reads=[xn, c['ident']], writes=[tp], inc=(dt == NDT - 1))
            for dt in range(NDT):
                do(P, 'dve', lambda e, dt=dt: e.tensor_scalar(out=hT.t[:, dt, col0:col0 + 128], in0=tp.t[:, dt, :],
                                                              scalar1=abcol.t[:, l, 2 * which, dt:dt + 1],
                                                              scalar2=abcol.t[:, l, 2 * which + 1, dt:dt + 1],
                                                              op0=ALU.mult, op1=ALU.add),
                   reads=[tp, abcol], writes=[hT])

        for l in range(DEPTH):
            fox = (l % 2 == 1)
            fl = l // 2
            xsrc = x_d if l == 0 else y_d
            SAB = Scope(P, "ab%d" % l)
            Vres = SAB.sb("Vres", [128, NT, D], BF16)
            if fox:
                nlf = SAB.sb("nlf", [128, NT, NH], F32)
                tab = SAB.sb("tab", [128, NB, NT, NH], F32)
            SA = Scope(P, "a%d" % l)
            Wqkv = SA.sb("Wqkv", [128, NDT, 3 * D], BF16)
            Wpart = [Buf(P, None, None, "Wp%d" % i) for i in range(3)]
            SA.bufs += Wpart
            for part in range(3):
                dma(P, 'sp', Wqkv.t[:, :, part * D:(part + 1) * D], wqkvb_d[:, :, part * D:(part + 1) * D],
                    reads=[wq_b], writes=[Wpart[part]], owner=Wpart[part])
            if fox:
                wfg = SA.sb("wfg", [128, NDT, NH], BF16)
                dma(P, 'pool', wfg.t[:], wfg_d[fl], writes=[wfg], owner=wfg)
                bfg = SA.sb("bfg", [128, NH], F32)
                dma(P, 'sp', bfg.t[:], bfg_d[fl].partition_broadcast(128), writes=[bfg], owner=bfg)
                fsb = [SA.sb("fsb%d" % i, [128, NH], F32) for i in range(2)]
            hT = [SA.sb("hT%d" % i, [128, NDT, 512], BF16) for i in range(2)]
            xt = [SA.sb("xt%d" % i, [128, D], F32) for i in range(3)]
            xn = [SA.sb("xn%d" % i, [128, D], BF16) for i in range(2)]
            junk = SA.sb("junk", [128, D], BF16)
            qks = [SA.sb("qks%d" % i, [128, 4, 512], BF16) for i in range(2)]
            SAP = Scope(P, "ap%d" % l)
            tp = [SAP.ps("tp%d" % i, [128, NDT, 128], BF16) for i in range(2)]
            mm = [SAP.ps("mm%d" % i, [128, 512], F32) for i in range(4)]
            nmm = 0
            nqk = 0
            if fox:
                flps = SAP.ps("flps", [128, NH], F32)
            def A_stages(tb, i):
                j = tb * 4 + i
                h_ = hT[tb % 2]
                x_ = xt[j % 3]
                xn_ = xn[j % 2]
                tp_ = tp[j % 2]
                stt = {}

                def s0():
                    dma(P, 'sp', x_.t[:], xsrc[j * 128:(j + 1) * 128, :], writes=[x_], owner=x_)
                    sb_ = stat[statn[0] % len(stat)]
                    statn[0] += 1
                    stt['sb'] = sb_
                    do(P, 'act', lambda e: e.activation(out=junk.t[:], in_=x_.t[:], func=AF.Square, accum_out=sb_.t[:, 0:1]),
                       reads=[x_], writes=[junk, sb_])

                def s1():
                    sb_ = stt['sb']
                    do(P, 'act', lambda e: e.activation(out=sb_.t[:, 1:2], in_=sb_.t[:, 0:1], func=AF.Ln, scale=1.0 / D, bias=EPS),
                       reads=[sb_], writes=[sb_])

                def s2():
                    sb_ = stt['sb']
                    do(P, 'act', lambda e: e.activation(out=sb_.t[:, 2:3], in_=sb_.t[:, 1:2], func=AF.Exp, scale=-0.5),
                       reads=[sb_], writes=[sb_])

                def s3():
                    sb_ = stt['sb']
                    do(P, 'dve', lambda e: e.tensor_scalar(out=xn_.t[:], in0=x_.t[:], scalar1=sb_.t[:, 2:3], scalar2=None, op0=ALU.mult),
                       reads=[x_, sb_], writes=[xn_])

                def s4():
                    for dt in range(NDT):
                        do(P, 'pe', lambda e, dt=dt: e.transpose(out=tp_.t[:, dt, :], in_=xn_.t[:, dt * 128:(dt + 1) * 128],
                                                                 identity=c['ident'].t[:]),
                           reads=[xn_, c['ident']], writes=[tp_], inc=(dt == NDT - 1))

                def s5():
                    for dt in range(NDT):
                        do(P, 'dve', lambda e, dt=dt: e.tensor_scalar(out=h_.t[:, dt, i * 128:(i + 1) * 128], in0=tp_.t[:, dt, :],
                                                                      scalar1=abcol.t[:, l, 0, dt:dt + 1],
                                                                      scalar2=abcol.t[:, l, 1, dt:dt + 1],
                                                                      op0=ALU.mult, op1=ALU.add),
                           reads=[tp_, abcol], writes=[h_])
                return [s0, s1, s2, s3, s4, s5]

            cntA = dict(mm=0, qk=0)

            def A_groups(tb):
                h_ = hT[tb % 2]
                groups = []
                for g in range(4):
                    for u in range(4):
                        def grp(g=g, u=u):
                            if u == 0:
                                cntA['qs'] = qks[cntA['qk'] % 2]
                                cntA['qk'] += 1
                            qs = cntA['qs']
                            et = g * 4 + u
                            m_ = mm[cntA['mm'] % 4]
                            cntA['mm'] += 1
                            for dt in range(NDT):
                                do(P, 'pe', lambda e, dt=dt: e.matmul(
                                    m_.t[:], lhsT=Wqkv.t[:, dt, et * 128:(et + 1) * 128], rhs=h_.t[:, dt, :],
                                    start=(dt == 0), stop=(dt == NDT - 1)),
                                   reads=[Wpart[et // 8], h_], writes=[m_], inc=(dt == NDT - 1))
                            do(P, 'act', lambda e: e.activation(out=qs.t[:, u, :], in_=m_.t[:], func=AF.Copy),
                               reads=[m_], writes=[qs])
                            if u == 3:
                                dst = (qT_d if g < 2 else kT_d)[(g % 2) * 512:(g % 2) * 512 + 512, tb * 512:(tb + 1) * 512]
                                dma(P, 'sp', dst.rearrange("(u p) t -> p u t", p=128), qs.t[:], reads=[qs], owner=qs)
                        groups.append(grp)
                for i in range(4):
                    j = tb * 4 + i
                    for hf in range(2):
                        def grp(i=i, j=j, hf=hf):
                            m_ = mm[cntA['mm'] % 4]
                            cntA['mm'] += 1
                            for dt in range(NDT):
                                do(P, 'pe', lambda e, dt=dt: e.matmul(
                                    m_.t[:], lhsT=h_.t[:, dt, i * 128:(i + 1) * 128],
                                    rhs=Wqkv.t[:, dt, 2 * D + hf * 512:2 * D + (hf + 1) * 512],
                                    start=(dt == 0), stop=(dt == NDT - 1)),
                                   reads=[Wpart[2], h_], writes=[m_], inc=(dt == NDT - 1))
                            do(P, 'dve', lambda e: e.tensor_copy(out=Vres.t[:, j, hf * 512:(hf + 1) * 512], in_=m_.t[:]),
                               reads=[m_], writes=[Vres])
                            if fox and hf == 1:
                                for dt in range(NDT):
                                    do(P, 'pe', lambda e, dt=dt: e.matmul(flps.t[:], lhsT=h_.t[:, dt, i * 128:(i + 1) * 128],
                                                                          rhs=wfg.t[:, dt, :], start=(dt == 0), stop=(dt == NDT - 1)),
                                       reads=[wfg, h_], writes=[flps], inc=(dt == NDT - 1))
                                f_ = fsb[j % 2]
                                do(P, 'dve', lambda e: e.tensor_tensor(out=f_.t[:], in0=flps.t[:], in1=bfg.t[:], op=ALU.add),
                                   reads=[flps, bfg], writes=[f_])
                                do(P, 'act', lambda e: e.activation(out=f_.t[:], in_=f_.t[:], func=AF.Exp, scale=-1.0),
                                   reads=[f_], writes=[f_])
                                do(P, 'act', lambda e: e.activation(out=nlf.t[:, j, :], in_=f_.t[:], func=AF.Ln, bias=1.0),
                                   reads=[f_], writes=[nlf])
                        groups.append(grp)
                return groups

            wavefront([A_stages(0, i) for i in range(4)], 2)
            for tb in range(NB):
                sched = {}
                if tb + 1 < NB:
                    for i_ in range(4):
                        for k2, fn_ in enumerate(A_stages(tb + 1, i_)):
                            sched.setdefault(4 * i_ + k2, []).append(fn_)
                for gi, grp in enumerate(A_groups(tb)):
                    grp()
                    for fn_ in sched.pop(gi, []):
                        fn_()
                assert not sched
            SAP.close()
            if fox:
                cps = SA.ps("cps", [128, NT * NH], F32)
                tps = SA.ps("tps", [128, NT * NH], F32)
                cumT = SA.sb("cumT", [128, NT, NH], F32)
                crefs = SA.sb("crefs", [128, NB, NH], F32)
                totT = SA.sb("totT", [128, NH, NT], F32)
                incl = SA.sb("incl", [128, NH, NT], F32)
                smask = SA.sb("smask", [128, NH, NT], F32)
                for j in range(NT):
                    do(P, 'pe', lambda e, j=j: e.matmul(cps.t[:, j * NH:(j + 1) * NH], lhsT=c['U_f'].t[:], rhs=nlf.t[:, j, :],
                                                        start=True, stop=True, skip_group_check=True),
                       reads=[nlf, c['U_f']], writes=[cps], inc=(j == NT - 1))
                for j in range(NT):
                    do(P, 'pe', lambda e, j=j: e.matmul(tps.t[:, j * NH:(j + 1) * NH], lhsT=c['ones_f'].t[:], rhs=nlf.t[:, j, :],
                                                        start=True, stop=True, skip_group_check=True),
                       reads=[nlf, c['ones_f']], writes=[tps], inc=(j == NT - 1))
                do(P, 'dve', lambda e: e.memset(smask.t[:], 1.0), writes=[smask])
                do(P, 'dve', lambda e: e.memset(smask.t[:, :, 0:1], 0.0), writes=[smask])
                do(P, 'dve', lambda e: e.tensor_copy(out=totT.t[:], in_=tps.t[:].rearrange("p (j h) -> p h j", h=NH)),
                   reads=[tps], writes=[totT])
                do(P, 'dve', lambda e: e.tensor_tensor_scan(out=incl.t[:].rearrange("p h j -> p (h j)"),
                                                            data0=smask.t[:].rearrange("p h j -> p (h j)"),
                                                            data1=totT.t[:].rearrange("p h j -> p (h j)"),
                                                            initial=0.0, op0=ALU.mult, op1=ALU.add),
                   reads=[smask, totT], writes=[incl])
                do(P, 'dve', lambda e: e.tensor_copy(out=crefs.t[:].rearrange("p q h -> p h q"),
                                                     in_=incl.t[:].rearrange("p h (q f) -> p h q f", f=4)[:, :, :, 3]),
                   reads=[incl], writes=[crefs])
                do(P, 'dve', lambda e: e.tensor_tensor(out=totT.t[:], in0=incl.t[:], in1=totT.t[:], op=ALU.subtract),
                   reads=[incl, totT], writes=[totT])
                do(P, 'dve', lambda e: e.tensor_tensor(out=cumT.t[:].rearrange("p j h -> p h j"),
                                                       in0=cps.t[:].rearrange("p (j h) -> p h j", h=NH), in1=totT.t[:], op=ALU.add),
                   reads=[cps, totT], writes=[cumT])
                for qb in range(NB):
                    nj = 4 * qb + 4
                    do(P, 'dve', lambda e, qb=qb, nj=nj: e.tensor_tensor(
                        out=tab.t[:, qb, 0:nj, :], in0=cumT.t[:, 0:nj, :],
                        in1=crefs.t[:, qb:qb + 1, :].to_broadcast([128, nj, NH]), op=ALU.subtract),
                       reads=[cumT, crefs], writes=[tab])
                HB = max(NB // 2, 1)
                HW = HB * 512
                trp = SA.ps("trp", [NH, HW], F32)
                dd = SA.sb("dd", [NH, HW], F32)
                dcol = SA.sb("dcol", [NH, HB], F32)
                dhi = [SA.sb("dhi%d" % i, [NH, HW], BF16) for i in range(1)]
                dlo = [SA.sb("dlo%d" % i, [NH, HW], BF16) for i in range(1)]
                for hf in range(NB // HB):
                    for i in range(HB * 4):
                        do(P, 'pe', lambda e, i=i, hf=hf: e.transpose(out=trp.t[:, i * 128:(i + 1) * 128],
                                                                      in_=cumT.t[:, hf * HB * 4 + i, :], identity=c['ident_f'].t[:]),
                           reads=[cumT, c['ident_f']], writes=[trp], inc=(i == HB * 4 - 1))
                    hi_ = dhi[0]
                    lo_ = dlo[0]
                    do(P, 'dve', lambda e: e.tensor_copy(out=dcol.t[:], in_=trp.t[:].rearrange("h (q f) -> h q f", f=512)[:, :, 511]),
                       reads=[trp], writes=[dcol])
                    do(P, 'dve', lambda e: e.tensor_tensor(out=dd.t[:].rearrange("h (q f) -> h q f", f=512),
                                                           in0=trp.t[:].rearrange("h (q f) -> h q f", f=512),
                                                           in1=dcol.t[:].unsqueeze(2).to_broadcast([NH, HB, 512]), op=ALU.subtract),
                       reads=[trp, dcol], writes=[dd])
                    do(P, 'dve', lambda e: e.tensor_scalar(out=dd.t[:], in0=dd.t[:], scalar1=-8.0, scalar2=None, op0=ALU.mult),
                       reads=[dd], writes=[dd])
                    do(P, 'dve', lambda e, hi_=hi_: e.tensor_copy(out=hi_.t[:], in_=dd.t[:]), reads=[dd], writes=[hi_])
                    do(P, 'dve', lambda e, hi_=hi_, lo_=lo_: e.tensor_tensor(out=lo_.t[:], in0=dd.t[:], in1=hi_.t[:], op=ALU.subtract),
                       reads=[dd, hi_], writes=[lo_])
                    dma(P, 'sp', aug_d[:, 0, hf * HW:(hf + 1) * HW], hi_.t[:], reads=[hi_], owner=hi_)
                    dma(P, 'sp', aug_d[:, 1, hf * HW:(hf + 1) * HW], lo_.t[:], reads=[lo_], owner=lo_)
            SA.close()
            if stop_after == ('A', l):
                SAB.close()
                break

            SB_ = Scope(P, "b%d" % l)
            bg = cast_ffn_list(l)
            if l + 1 < DEPTH:
                bg += cast_wqkv_list(l + 1)
            if fox:
                strm = [(AttnStream(SB_, S, True, "a", 3, 0, 2), list(range(NH)))]
            else:
                strm = [(AttnStream(SB_, S, False, "a", 2, 1, 1), list(range(0, NH, 2))),
                        (AttnStream(SB_, S, False, "b", 2, 1, 1), list(range(1, NH, 2)))]
            if fox:
                for AB, _h in strm:
                    for sl in range(2):
                        do(P, 'dve', lambda e, sl=sl, AB=AB: e.memset(AB.kT[sl].t[64:66, :], 1.0), writes=[AB.kT[sl]])
                        do(P, 'pool', lambda e, sl=sl, AB=AB: e.memset(AB.VO[sl].t[:, :, 64:128], 1.0), writes=[AB.VO[sl]])

            def load_head(AB, h, sl, fox=fox, Vres=Vres):
                dma(P, 'sp', AB.qT[sl].t[0:64, :], qT_d[h * 64:(h + 1) * 64, :], writes=[AB.qT[sl]], owner=AB.qT[sl])
                dma(P, 'sp', AB.kT[sl].t[0:64, :], kT_d[h * 64:(h + 1) * 64, :], writes=[AB.kT[sl]], owner=AB.kT[sl])
                if fox:
                    dma(P, 'sp', AB.qT[sl].t[64:66, :], aug_d[h], writes=[AB.qT[sl]], owner=AB.qT[sl])
                    do(P, 'pool', lambda e: e.tensor_copy(out=AB.VO[sl].t[:, :, 0:64], in_=Vres.t[:, :, h * 64:(h + 1) * 64]),
                       reads=[Vres], writes=[AB.VO[sl]])

            def store_q(oq, h, qb):
                dma(P, 'sp', oT_d[h * 64:(h + 1) * 64, qb * QB:(qb + 1) * QB], oq.t[:], reads=[oq], owner=oq)

            def v_of(AB, h, sl, j, Vres=Vres, fox=fox):
                if fox:
                    return AB.VO[sl].t[:, j, :], AB.VO[sl]
                return Vres.t[:, j, h * 64:(h + 1) * 64], Vres

            bias_of = None
            if fox:
                def bias_of(h, qb, j, tab=tab):
                    return tab.t[:, qb, j, h:h + 1], tab
            emit_attention(P, strm, c, fox, load_head, store_q, v_of, bias_of, background=bg)
            SB_.close()
            SAB.close()
            if stop_after == ('B', l):
                break

            SC = Scope(P, "c%d" % l)
            Wo = SC.sb("Wo", [128, NDT, D], BF16)
            dma(P, 'sp', Wo.t[:], wob_d, reads=[wo_b], writes=[Wo], owner=Wo)
            Wd = SC.sb("Wd", [128, NFT, D], BF16)
            wcv = SC.sb("wcv", [128, NFT, 3], F32)
            bcv = SC.sb("bcv", [128, NFT], F32)
            dma(P, 'sp', wcv.t[:], wcv_d[l], writes=[wcv], owner=wcv)
            dma(P, 'sp', bcv.t[:], bcv_d[l], writes=[bcv], owner=bcv)
            GG = SC.sb("GG", [128, 2, D], F32)
            for which in range(2):
                dma(P, 'sp', GG.t[:, which, :], gg_d[l, which], writes=[GG], owner=GG)
            halo = SC.sb("halo", [128, NFT, 2], F32)
            do(P, 'dve', lambda e: e.memset(halo.t[:], 0.0), writes=[halo])
            oTb = SC.sb("oTb", [128, NDT, 512], BF16)
            xt = [SC.sb("xt%d" % i, [128, D], F32) for i in range(3)]
            xnw = [SC.sb("xnw%d" % i, [128, D], F32) for i in range(3)]
            xn = [SC.sb("xn%d" % i, [128, D], BF16) for i in range(2)]
            junk = SC.sb("junk", [128, D], BF16)
            t1 = [SC.sb("t1%d" % i, [128, D], F32) for i in range(2)]
            xo = [SC.sb("xo%d" % i, [128, D], F32) for i in range(2)]
            h2T = [SC.sb("h2T%d" % i, [128, NDT, 512], BF16) for i in range(2)]
            aT = SC.sb("aT", [128, NFT, 512], BF16)
            wgr = [SC.sb("wg%d" % i, [128, NDT, 128], BF16) for i in range(4)]
            wur = [SC.sb("wu%d" % i, [128, NDT, 128], BF16) for i in range(4)]
            gbuf = [SC.sb("gbuf%d" % i, [128, 514], F32) for i in range(2)]
            cv = [SC.sb("cv%d" % i, [128, 512], F32) for i in range(3)]
            sg = [SC.sb("sg%d" % i, [128, 512], F32) for i in range(3)]
            tp = SC.ps("tp", [128, NDT, 128], BF16)
            yps = [SC.ps("yps%d" % i, [128, D], F32) for i in range(2)]
            gps = [SC.ps("gps%d" % i, [128, 512], F32) for i in range(2)]
            ups = SC.ps("ups", [128, 512], F32)
            ydr = [Buf(P, None, None, "ydr%d" % j) for j in range(NT)]
            cnt = dict(y=0, t1=0, xt=0, xnw=0, xn=0, xo=0)
            Tstate = {}

            def T_stages(tb, i):
                j = tb * 4 + i
                hT_ = h2T[tb % 2]
                stt = {}

                def s0():
                    y_ = yps[cnt['y'] % 2]
                    cnt['y'] += 1
                    stt['y'] = y_
                    for hf in range(2):
                        for et in range(NDT):
                            do(P, 'pe', lambda e, et=et, hf=hf: e.matmul(
                                y_.t[:, hf * 512:(hf + 1) * 512], lhsT=oTb.t[:, et, i * 128:(i + 1) * 128],
                                rhs=Wo.t[:, et, hf * 512:(hf + 1) * 512], start=(et == 0), stop=(et == NDT - 1)),
                               reads=[oTb, Wo], writes=[y_], inc=(et == NDT - 1 and hf == 1))
                    x_ = xt[cnt['xt'] % 3]
                    cnt['xt'] += 1
                    stt['x'] = x_
                    dma(P, 'pool', x_.t[:], xsrc[j * 128:(j + 1) * 128, :], reads=[ydr[j]] if l > 0 else [], writes=[x_], owner=x_)

                def mk_stats(key_src, key_out):
                    def a():
                        src = stt[key_src]
                        sb_ = stat[statn[0] % len(stat)]
                        statn[0] += 1
                        stt[key_out] = sb_
                        do(P, 'act', lambda e: e.activation(out=junk.t[:], in_=src.t[:], func=AF.Square, accum_out=sb_.t[:, 0:1]),
                           reads=[src], writes=[junk, sb_])

                    def b():
                        sb_ = stt[key_out]
                        do(P, 'act', lambda e: e.activation(out=sb_.t[:, 1:2], in_=sb_.t[:, 0:1], func=AF.Ln, scale=1.0 / D, bias=EPS),
                           reads=[sb_], writes=[sb_])

                    def c_():
                        sb_ = stt[key_out]
                        do(P, 'act', lambda e: e.activation(out=sb_.t[:, 2:3], in_=sb_.t[:, 1:2], func=AF.Exp, scale=-0.5),
                           reads=[sb_], writes=[sb_])
                    return [a, b, c_]

                def s4():
                    t_ = t1[cnt['t1'] % 2]
                    cnt['t1'] += 1
                    stt['t'] = t_
                    y_ = stt['y']
                    sb_ = stt['st1']
                    do(P, 'act', lambda e: e.activation(out=t_.t[:], in_=y_.t[:], func=AF.Copy, scale=sb_.t[:, 2:3]),
                       reads=[y_, sb_], writes=[t_])

                def s5():
                    xw = xnw[cnt['xnw'] % 3]
                    cnt['xnw'] += 1
                    stt['xw'] = xw
                    t_ = stt['t']
                    x_ = stt['x']
                    do(P, 'pool', lambda e: e.tensor_tensor(out=t_.t[:], in0=t_.t[:], in1=GG.t[:, 0, :], op=ALU.mult),
                       reads=[GG], writes=[t_])
                    do(P, 'pool', lambda e: e.tensor_tensor(out=xw.t[:], in0=t_.t[:], in1=x_.t[:], op=ALU.add),
                       reads=[t_, x_], writes=[xw])
                    dma(P, 'pool', y_d[j * 128:(j + 1) * 128, :], xw.t[:], reads=[xw], writes=[ydr[j]], owner=xw)

                def s9():
                    xn_ = xn[cnt['xn'] % 2]
                    cnt['xn'] += 1
                    stt['xn'] = xn_
                    xw = stt['xw']
                    sb2 = stt['st2']
                    do(P, 'dve', lambda e: e.tensor_scalar(out=xn_.t[:], in0=xw.t[:], scalar1=sb2.t[:, 2:3], scalar2=None, op0=ALU.mult),
                       reads=[xw, sb2], writes=[xn_])

                def s10():
                    xn_ = stt['xn']
                    for dt in range(NDT):
                        do(P, 'pe', lambda e, dt=dt: e.transpose(out=tp.t[:, dt, :], in_=xn_.t[:, dt * 128:(dt + 1) * 128],
                                                                 identity=c['ident'].t[:]),
                           reads=[xn_, c['ident']], writes=[tp], inc=(dt == NDT - 1))

                def s11():
                    for dt in range(NDT):
                        do(P, 'dve', lambda e, dt=dt: e.tensor_scalar(out=hT_.t[:, dt, i * 128:(i + 1) * 128], in0=tp.t[:, dt, :],
                                                                      scalar1=abcol.t[:, l, 2, dt:dt + 1],
                                                                      scalar2=abcol.t[:, l, 3, dt:dt + 1],
                                                                      op0=ALU.mult, op1=ALU.add),
                           reads=[tp, abcol], writes=[hT_])
                return [s0] + mk_stats('y', 'st1') + [s4, s5] + mk_stats('xw', 'st2') + [s9, s10, s11]

            def load_oTb(tb):
                dma(P, 'sp', oTb.t[:], oT_d.rearrange("(et p) t -> p et t", p=128)[:, :, tb * 512:(tb + 1) * 512],
                    writes=[oTb], owner=oTb)

            nw = [0]

            def F1(tb):
                hT_ = h2T[tb % 2]
                sched = {}
                if tb + 1 < NB:
                    for i_ in range(4):
                        for k2, fn_ in enumerate(T_stages(tb + 1, i_)):
                            sched.setdefault(8 * i_ + k2, []).append(fn_)
                info = {}
                for it in range(NFT + 2):
                    if it < NFT:
                        ft = it
                        k_ = nw[0]
                        nw[0] += 1
                        wg_ = wgr[k_ % 4]
                        wu_ = wur[k_ % 4]
                        g_ = gps[k_ % 2]
                        gb = gbuf[k_ % 2]
                        cv_ = cv[k_ % 3]
                        sg_ = sg[k_ % 3]
                        info[ft] = (wu_, cv_, sg_)
                        dma(P, 'sp', wg_.t[:], wgb_d[ft], reads=[wg_b], writes=[wg_], owner=wg_)
                        dma(P, 'sp', wu_.t[:], wub_d[ft], reads=[wu_b], writes=[wu_], owner=wu_)
                        for dt in range(NDT):
                            do(P, 'pe', lambda e, dt=dt: e.matmul(g_.t[:], lhsT=wg_.t[:, dt, :], rhs=hT_.t[:, dt, :],
                                                                  start=(dt == 0), stop=(dt == NDT - 1)),
                               reads=[wg_, hT_], writes=[g_], inc=(dt == NDT - 1))
                        do(P, 'act', lambda e: e.activation(out=gb.t[:, 2:514], in_=g_.t[:], func=AF.Copy),
                           reads=[g_], writes=[gb])
                        do(P, 'act', lambda e: e.activation(out=cv_.t[:], in_=g_.t[:], func=AF.Identity,
                                                            scale=wcv.t[:, ft, 2:3], bias=bcv.t[:, ft:ft + 1]),
                           reads=[g_, wcv, bcv], writes=[cv_])
                        do(P, 'act', lambda e: e.activation(out=gb.t[:, 0:2], in_=halo.t[:, ft, :], func=AF.Copy),
                           reads=[halo], writes=[gb])
                        do(P, 'act', lambda e: e.activation(out=halo.t[:, ft, :], in_=g_.t[:, 510:512], func=AF.Copy),
                           reads=[g_], writes=[halo])
                        do(P, 'dve', lambda e: e.scalar_tensor_tensor(
                            out=cv_.t[:], in0=gb.t[:, 1:513], scalar=wcv.t[:, ft, 1:2], in1=cv_.t[:],
                            op0=ALU.mult, op1=ALU.add), reads=[gb, wcv, cv_], writes=[cv_])
                        do(P, 'dve', lambda e: e.scalar_tensor_tensor(
                            out=cv_.t[:], in0=gb.t[:, 0:512], scalar=wcv.t[:, ft, 0:1], in1=cv_.t[:],
                            op0=ALU.mult, op1=ALU.add), reads=[gb, wcv, cv_], writes=[cv_])
                    if 1 <= it <= NFT:
                        _wu, pcv, psg = info[it - 1]
                        do(P, 'act', lambda e: e.activation(out=psg.t[:], in_=pcv.t[:], func=AF.Exp, scale=-1.0),
                           reads=[pcv], writes=[psg])
                        do(P, 'act', lambda e: e.activation(out=psg.t[:], in_=psg.t[:], func=AF.Ln, bias=1.0),
                           reads=[psg], writes=[psg])
                        do(P, 'act', lambda e: e.activation(out=psg.t[:], in_=psg.t[:], func=AF.Exp, scale=-1.0),
                           reads=[psg], writes=[psg])
                        do(P, 'pool', lambda e: e.tensor_tensor(out=psg.t[:], in0=psg.t[:], in1=pcv.t[:], op=ALU.mult),
                           reads=[pcv], writes=[psg])
                    for fn_ in sched.pop(2 * it, []):
                        fn_()
                    if 2 <= it:
                        pft = it - 2
                        pwu, _cv, psg = info.pop(pft)
                        for dt in range(NDT):
                            do(P, 'pe', lambda e, dt=dt: e.matmul(ups.t[:], lhsT=pwu.t[:, dt, :], rhs=hT_.t[:, dt, :],
                                                                  start=(dt == 0), stop=(dt == NDT - 1)),
                               reads=[pwu, hT_], writes=[ups], inc=(dt == NDT - 1))
                        do(P, 'dve', lambda e: e.tensor_tensor(out=aT.t[:, pft, :], in0=ups.t[:], in1=psg.t[:], op=ALU.mult),
                           reads=[psg, ups], writes=[aT])
                    for fn_ in sched.pop(2 * it + 1, []):
                        fn_()
                assert not sched

            def F2(tb, i):
                j = tb * 4 + i
                y_ = yps[cnt['y'] % 2]
                cnt['y'] += 1
                for hf in range(2):
                    for ft in range(NFT):
                        do(P, 'pe', lambda e, ft=ft, hf=hf: e.matmul(
                            y_.t[:, hf * 512:(hf + 1) * 512], lhsT=aT.t[:, ft, i * 128:(i + 1) * 128],
                            rhs=Wd.t[:, ft, hf * 512:(hf + 1) * 512], start=(ft == 0), stop=(ft == NFT - 1)),
                           reads=[aT, Wd], writes=[y_], inc=(ft == NFT - 1 and hf == 1))
                x_ = xt[cnt['xt'] % 3]
                cnt['xt'] += 1
                dma(P, 'pool', x_.t[:], y_d[j * 128:(j + 1) * 128, :], reads=[ydr[j]], writes=[x_], owner=x_)
                rstd, sb_ = rms_stats(y_.t[:], y_, junk, 1.0 / D)
                t_ = t1[cnt['t1'] % 2]
                cnt['t1'] += 1
                do(P, 'dve', lambda e: e.scalar_tensor_tensor(out=t_.t[:], in0=y_.t[:], scalar=rstd, in1=GG.t[:, 1, :],
                                                              op0=ALU.mult, op1=ALU.mult),
                   reads=[y_, sb_, GG], writes=[t_])
                xo_ = xo[cnt['xo'] % 2]
                cnt['xo'] += 1
                do(P, 'pool', lambda e: e.tensor_tensor(out=xo_.t[:], in0=t_.t[:], in1=x_.t[:], op=ALU.add),
                   reads=[t_, x_], writes=[xo_])
                dma(P, 'pool', y_d[j * 128:(j + 1) * 128, :], xo_.t[:], reads=[xo_], writes=[ydr[j]], owner=xo_)

            load_oTb(0)
            for q4 in range(0, NFT, 11):
                dma(P, 'sp', Wd.t[:, q4:q4 + 11, :], wdb_d[:, q4:q4 + 11, :], reads=[wd_b], writes=[Wd], owner=Wd)
            wavefront([T_stages(0, i) for i in range(4)], 3)
            for tb in range(NB):
                if tb + 1 < NB:
                    load_oTb(tb + 1)
                F1(tb)
                for i in range(4):
                    F2(tb, i)
            SC.close()
        G.close()
    return nc, P


def make_consts():
    kp = np.arange(128)[:, None]
    qf = np.arange(128)[None, :]
    ident = np.eye(128, dtype=np.float32)
    msk_sb = np.where(kp < qf, 0.0, NEG).astype(np.float32)
    msk_fox = np.where(kp <= qf, 0.0, NEG).astype(np.float32)
    Linc = (kp >= qf).astype(np.float32)
    Lcomp = (1.0 - Linc).astype(np.float32)
    U = (kp <= qf).astype(np.float32)
    ones = np.ones((128, 128), np.float32)
    return np.stack([ident, msk_sb, msk_fox, Linc, Lcomp, U, ones]).astype(np.float32)


def layout_inputs(inp, S, DEPTH):
    f = lambda a: np.ascontiguousarray(np.asarray(a, dtype=np.float32))
    L = DEPTH
    NF = max(DEPTH // 2, 1)
    shared = {}
    wm_ = np.asarray(inp["w_mod"])[:L]
    shared["w_mod"] = f(wm_.reshape(L, NDT, 128, 6 * D).transpose(0, 2, 1, 3))
    ng = np.concatenate([wm_[:, :, 0:2 * D], wm_[:, :, 3 * D:5 * D]], axis=2)
    shared["w_modT"] = f(ng.transpose(0, 2, 1).reshape(L, 32, 128, D).transpose(0, 2, 1, 3))
    bm = np.asarray(inp["b_mod"])[:L]
    shared["b_mod_col"] = f(bm.reshape(L, 48, 128).transpose(0, 2, 1))
    shared["b_mod"] = f(bm)
    gpre = np.stack([np.asarray(inp["g_mix_pre"])[:L], np.asarray(inp["g_ffn_pre"])[:L]], axis=1)
    shared["g_pre"] = f(gpre.reshape(L, 2, NDT, 128).transpose(0, 3, 1, 2))
    shared["g_post"] = f(np.stack([np.asarray(inp["g_mix_post"])[:L], np.asarray(inp["g_ffn_post"])[:L]], axis=1))
    shared["w_qkv"] = f(np.asarray(inp["w_qkv"])[:L].reshape(L, NDT, 128, 3 * D).transpose(0, 2, 1, 3))
    shared["w_o"] = f(np.asarray(inp["w_o"])[:L].reshape(L, NDT, 128, D).transpose(0, 2, 1, 3))
    wfg = np.asarray(inp["w_fg"])
    bfg = np.asarray(inp["b_fg"])
    shared["w_fg"] = f(wfg[:NF].reshape(NF, NDT, 128, NH).transpose(0, 2, 1, 3))
    shared["b_fg"] = f(bfg[:NF])
    for k, nm in (("w_ffn_gate", "w_g"), ("w_ffn_up", "w_u")):
        shared[nm] = f(np.asarray(inp[k])[:L].reshape(L, NDT, 128, NFT, 128).transpose(0, 3, 2, 1, 4))
    shared["w_d"] = f(np.asarray(inp["w_ffn_down"])[:L].reshape(L, NFT, 128, D).transpose(0, 2, 1, 3))
    shared["w_conv"] = f(np.asarray(inp["w_conv"])[:L].reshape(L, 3, NFT, 128).transpose(0, 3, 2, 1))
    shared["b_conv"] = f(np.asarray(inp["b_conv"])[:L].reshape(L, NFT, 128).transpose(0, 2, 1))
    shared["consts"] = make_consts()
    x = np.asarray(inp["x"])
    cc = np.asarray(inp["c"])
    maps = []
    for b in range(x.shape[0]):
        m = dict(shared)
        m["x"] = f(x[b, :S])
        m["c"] = f(cc[b].reshape(NDT, 128).T)
        m["c_row"] = f(cc[b].reshape(1, D))
        maps.append(m)
    return maps


def kernel(**inputs):
    x = np.asarray(inputs["x"])
    B, S, _ = x.shape
    DEPTH = np.asarray(inputs["w_qkv"]).shape[0]
    nc, _ = build_program(S, DEPTH)
    maps = layout_inputs(inputs, S, DEPTH)
    res = run_bass_kernel_spmd(nc, maps, core_ids=list(range(B)))
    out = np.stack([np.asarray(res.results[b]["y"], dtype=np.float32) for b in range(B)], axis=0)
    return out
```
